# Optimizing a Trainium2 kernel written in Bass

```python
import jax, jax.numpy as jnp
from jax import lax
import numpy as np

D_MODEL = 2048
BATCH = 32
SEQ = 256
DEPTH = 1
DEC_BATCH = 8
DEC_SEQ = 4096
PAST_LEN = 512

GRID_W = 64
HG_WIDTH = D_MODEL // 2
HG_DK = 128
HG_DV = 128
HG_HEADS = HG_WIDTH // HG_DK
POOL_WIDTH = D_MODEL - HG_WIDTH
POOL_WINDOWS = (2, 4, 8, 16)
N_POOL_GROUPS = len(POOL_WINDOWS)
POOL_GROUP = POOL_WIDTH // N_POOL_GROUPS
MIX_WIDTH = HG_WIDTH + POOL_WIDTH
IN_WIDTH = 5 * HG_WIDTH + POOL_WIDTH
D_FF = 5632
CONV_W = 3
CHUNK = 32
N_MOD = 6
EPS = 1e-6

kernel_name = "hybrid_hgrn2_pool_dit_step"


def rmsnorm(x, w):
    xf = x.astype(jnp.float32)
    y = xf * lax.rsqrt(jnp.mean(xf * xf, axis=-1, keepdims=True) + EPS)
    return (y * w.astype(jnp.float32)).astype(x.dtype)


def ada_mod(cvec, w_ada, b_ada):
    m = jax.nn.silu(cvec) @ w_ada + b_ada
    if m.ndim == 2:
        m = m[:, None, :]
    return jnp.split(m, N_MOD, axis=-1)


def to_heads(a):
    B, L, _ = a.shape
    return a.reshape(B, L, HG_HEADS, -1).transpose(0, 2, 1, 3)


def chunk_scan(q, k, v, log_f, s0):
    B, H, L, DK = q.shape
    n = L // CHUNK

    def to_chunks(a):
        return jnp.moveaxis(a.reshape(B, H, n, CHUNK, a.shape[-1]), 2, 0)

    mask = jnp.tril(jnp.ones((CHUNK, CHUNK), dtype=bool))[:, :, None]

    def step(S, inp):
        qc, kc, vc, gc = inp
        b = jnp.cumsum(gc, axis=2)
        diff = b[:, :, :, None, :] - b[:, :, None, :, :]
        decay = jnp.exp(jnp.where(mask, diff, -jnp.inf))
        scores = jnp.einsum('bhtd,bhsd,bhtsd->bhts', qc, kc, decay)
        o = (jnp.einsum('bhts,bhsv->bhtv', scores, vc)
             + jnp.einsum('bhtd,bhdv->bhtv', qc * jnp.exp(b), S))
        bl = b[:, :, -1:, :]
        S = (jnp.exp(bl[:, :, 0, :])[..., None] * S
             + jnp.einsum('bhsd,bhsv->bhdv', kc * jnp.exp(bl - b), vc))
        return S, o

    S, o = lax.scan(step, s0, (to_chunks(q), to_chunks(k), to_chunks(v), to_chunks(log_f)))
    o = jnp.moveaxis(o, 0, 2).reshape(B, H, L, v.shape[-1])
    return o, S


def hgrn2_bidir(q, zf, zb, v, lb, s0):
    f32 = jnp.float32
    q = jax.nn.silu(q.astype(f32))
    v = v.astype(f32)
    s0 = s0.astype(f32)

    def gates(z, lbd):
        lbd = lbd.reshape(HG_HEADS, 1, HG_DK)
        f = lbd + (1.0 - lbd) * jax.nn.sigmoid(z.astype(f32))
        return jnp.log(f), 1.0 - f

    logf_f, k_f = gates(zf, lb[0])
    logf_b, k_b = gates(zb, lb[1])
    rev = lambda a: jnp.flip(a, axis=2)
    o_f, S_f = chunk_scan(q, k_f, v, logf_f, s0[:, 0])
    o_b, S_b = chunk_scan(rev(q), rev(k_b), rev(v), rev(logf_b), s0[:, 1])
    return o_f + rev(o_b), jnp.stack([S_f, S_b], axis=1)


def centred_mean(a, w, axis):
    L = a.shape[axis]
    t = jnp.arange(L)
    lo = jnp.clip(t - w // 2, 0, L)
    hi = jnp.clip(t + (w - w // 2), 0, L)
    pad = [(0, 0)] * a.ndim
    pad[axis] = (1, 0)
    cs = jnp.pad(jnp.cumsum(a, axis=axis), pad)
    s = jnp.take(cs, hi, axis=axis) - jnp.take(cs, lo, axis=axis)
    shape = [1] * a.ndim
    shape[axis] = L
    cnt = (hi - lo).astype(a.dtype).reshape(shape)
    return s / cnt


def pool_mixer(p, w_pool, pool_scale, on_grid):
    B, L, _ = p.shape
    pf = p.astype(jnp.float32)
    outs = []
    for gi, w in enumerate(POOL_WINDOWS):
        pg = pf[..., gi * POOL_GROUP:(gi + 1) * POOL_GROUP]
        if on_grid:
            rows = L // GRID_W
            pgg = pg.reshape(B, rows, GRID_W, POOL_GROUP)
            m = centred_mean(centred_mean(pgg, w, 2), w, 1).reshape(B, L, POOL_GROUP)
        else:
            m = centred_mean(pg, w, 1)
        outs.append((m - pg).astype(p.dtype) @ w_pool[gi])
    return jnp.concatenate(outs, axis=-1) * pool_scale


def token_mixer(h, s0, lb, w_in, hg_norm_w, w_pool, pool_scale, w_out, on_grid):
    B, L, _ = h.shape
    proj = h @ w_in
    q, zf, zb, v, g, p = jnp.split(proj, [HG_WIDTH * i for i in range(1, 6)], axis=-1)
    o, S = hgrn2_bidir(to_heads(q), to_heads(zf), to_heads(zb), to_heads(v), lb, s0)
    o = o * lax.rsqrt(jnp.mean(o * o, axis=-1, keepdims=True) + EPS) * hg_norm_w.astype(jnp.float32)
    o = o * jax.nn.silu(to_heads(g).astype(jnp.float32))
    o = o.transpose(0, 2, 1, 3).reshape(B, L, HG_WIDTH).astype(h.dtype)
    pm = pool_mixer(p, w_pool, pool_scale, on_grid)
    return jnp.concatenate([o, pm], axis=-1) @ w_out, S


def conv_ffn(h, w_up, conv_w, conv_b, w_down):
    u = h @ w_up
    up = jnp.pad(u, ((0, 0), (1, 1), (0, 0)))
    u = up[:, :-2] * conv_w[0] + up[:, 1:-1] * conv_w[1] + up[:, 2:] * conv_w[2] + conv_b
    a, b = jnp.split(u, 2, axis=-1)
    return (jax.nn.silu(a) * b) @ w_down


def trunk_layer(x, mod, s0, lb, norm1_w, w_in, hg_norm_w, w_pool, pool_scale, w_out,
                norm2_w, w_up, conv_w, conv_b, w_down, on_grid):
    sh1, sc1, g1, sh2, sc2, g2 = mod
    h = rmsnorm(x, norm1_w) * (1.0 + sc1) + sh1
    m, S = token_mixer(h, s0, lb, w_in, hg_norm_w, w_pool, pool_scale, w_out, on_grid)
    x = x + g1 * m
    h = rmsnorm(x, norm2_w) * (1.0 + sc2) + sh2
    x = x + g2 * conv_ffn(h, w_up, conv_w, conv_b, w_down)
    return x, S


def setup_inputs(seed: int = 0) -> dict:
    key = jax.random.key(seed)
    ks = jax.random.split(key, 20)
    f32 = jnp.float32
    nrm = lambda k, shape, s: jax.random.normal(k, shape, f32) * s
    return {
        "x_prompt": nrm(ks[0], (BATCH, SEQ, D_MODEL), 1.0),
        "x_sample": nrm(ks[1], (DEC_BATCH, DEC_SEQ, D_MODEL), 1.0),
        "state_hgrn": nrm(ks[2], (DEC_BATCH, DEPTH, 2, HG_HEADS, HG_DK, HG_DV), 0.3),
        "c": nrm(ks[3], (DEC_BATCH, D_MODEL), 1.0),
        "c_ctx": nrm(ks[4], (D_MODEL,), 1.0),
        "w_ada": nrm(ks[5], (DEPTH, D_MODEL, N_MOD * D_MODEL), 0.5 * D_MODEL ** -0.5),
        "b_ada": nrm(ks[6], (DEPTH, N_MOD * D_MODEL), 0.02),
        "norm1_w": 1.0 + nrm(ks[7], (DEPTH, D_MODEL), 0.02),
        "w_in": nrm(ks[8], (DEPTH, D_MODEL, IN_WIDTH), D_MODEL ** -0.5),
        "lb_param": nrm(ks[9], (2, DEPTH + 1, HG_WIDTH), 0.5),
        "hg_norm_w": 1.0 + nrm(ks[10], (DEPTH, HG_DV), 0.02),
        "w_pool": nrm(ks[11], (DEPTH, N_POOL_GROUPS, POOL_GROUP, POOL_GROUP), POOL_GROUP ** -0.5),
        "pool_scale": 1.0 + nrm(ks[12], (DEPTH, POOL_WIDTH), 0.02),
        "w_out": nrm(ks[13], (DEPTH, MIX_WIDTH, D_MODEL), MIX_WIDTH ** -0.5),
        "norm2_w": 1.0 + nrm(ks[14], (DEPTH, D_MODEL), 0.02),
        "w_up": nrm(ks[15], (DEPTH, D_MODEL, 2 * D_FF), D_MODEL ** -0.5),
        "conv_w": nrm(ks[16], (DEPTH, CONV_W, 2 * D_FF), CONV_W ** -0.5),
        "conv_b": nrm(ks[17], (DEPTH, 2 * D_FF), 0.02),
        "w_down": nrm(ks[18], (DEPTH, D_FF, D_MODEL), D_FF ** -0.5),
        "final_norm_w": 1.0 + nrm(ks[19], (D_MODEL,), 0.02),
    }


def reference(x_prompt, x_sample, state_hgrn, c, c_ctx, w_ada, b_ada, norm1_w, w_in,
              lb_param, hg_norm_w, w_pool, pool_scale, w_out, norm2_w, w_up, conv_w,
              conv_b, w_down, final_norm_w):
    lb_all = jnp.cumsum(jax.nn.softmax(lb_param.astype(jnp.float32), axis=1), axis=1)
    xp, xs = x_prompt, x_sample
    ctx_states = []
    for l in range(DEPTH):
        lb = lb_all[:, l]
        weights = (norm1_w[l], w_in[l], hg_norm_w[l], w_pool[l], pool_scale[l], w_out[l],
                   norm2_w[l], w_up[l], conv_w[l], conv_b[l], w_down[l])
        s0 = jnp.zeros((xp.shape[0], 2, HG_HEADS, HG_DK, HG_DV), jnp.float32)
        xp, S_ctx = trunk_layer(xp, ada_mod(c_ctx, w_ada[l], b_ada[l]), s0, lb, *weights,
                                on_grid=False)
        ctx_states.append(S_ctx.astype(x_prompt.dtype))
        xs, _ = trunk_layer(xs, ada_mod(c, w_ada[l], b_ada[l]), state_hgrn[:, l], lb, *weights,
                            on_grid=True)
    y_prompt = rmsnorm(xp, final_norm_w)
    y_sample = rmsnorm(xs, final_norm_w)
    new_state_hgrn = jnp.stack(ctx_states, axis=1)
    return (y_prompt, y_sample, new_state_hgrn)
```

```python
import numpy as np
import ml_dtypes
import concourse.bass as bass
import concourse.mybir as mybir
from concourse.bass_utils import run_bass_kernel_spmd

F32 = mybir.dt.float32
BF16 = mybir.dt.bfloat16
AF = mybir.ActivationFunctionType
ALU = mybir.AluOpType

EPS = 1e-6
D = 2048
NKC = 16
HEADS = 8
DFF = 5632
NFC = 44
TT = 512
CH = 64
NDMASEM = 8


class Prog:
    ENGS = ("pe", "act", "dve", "pool", "sp")

    def __init__(self, nc):
        self.nc = nc
        self.eng = {"pe": nc.tensor, "act": nc.scalar, "dve": nc.vector, "pool": nc.gpsimd, "sp": nc.sync}
        self.ops = []
        self.ncomp = {e: 0 for e in self.ENGS}
        self.comp_ops = {e: [] for e in self.ENGS}
        self.writers = {}
        self.readers = {}
        self.known = {e: {} for e in self.ENGS}
        self.known_dma = {e: set() for e in self.ENGS}
        self.ndma = {"sp": 0, "pool": 0, "act": 0}
        self.bg = set()

    def _deps(self, eng, is_dma, r, w):
        raw = []
        other = []
        for k in r:
            raw += self.writers.get(k, [])
        for k in w:
            raw += self.writers.get(k, [])
            other += self.readers.get(k, [])
        waits = []
        best = {}
        dmas = []
        for kind, toks in (("raw", raw), ("war", other)):
            for t in toks:
                if t[0] == "c":
                    _, e2, idx = t
                    if e2 == eng and not is_dma:
                        if eng == "pe":
                            continue
                    if idx > best.get(e2, -1):
                        best[e2] = idx
                else:
                    dmas.append(t)
        for e2, idx in best.items():
            if self.known[eng].get(e2, -1) >= idx:
                continue
            self.known[eng][e2] = idx
            waits.append(("c", e2, idx))
        for t in dmas:
            if t in self.known_dma[eng]:
                continue
            self.known_dma[eng].add(t)
            waits.append(t)
        return waits

    def _commit(self, tok, r, w):
        for k in w:
            self.writers[k] = [tok]
            self.readers[k] = []
        for k in r:
            self.readers.setdefault(k, []).append(tok)

    LIMIT = None

    def op(self, eng, fn, r=(), w=()):
        if Prog.LIMIT is not None and len(self.ops) >= Prog.LIMIT:
            return None
        waits = self._deps(eng, False, r, w)
        idx = self.ncomp[eng]
        self.ncomp[eng] += 1
        tok = ("c", eng, idx)
        rec = dict(eng=eng, fn=fn, waits=waits, dma=None, inc=False, idx=idx)
        self.ops.append(rec)
        self.comp_ops[eng].append(rec)
        self._commit(tok, r, w)
        return tok

    def dma(self, q, out, in_, r=(), w=(), bg=False, **kw):
        if Prog.LIMIT is not None and len(self.ops) >= Prog.LIMIT:
            return None
        waits = self._deps(q, True, r, w)
        n = self.ndma[q]
        self.ndma[q] += 1
        tok = ("d", q, n)
        if bg:
            self.bg.add(tok)
        if n >= NDMASEM:
            prev = ("d", q, n - NDMASEM)
            if prev not in self.known_dma[q]:
                self.known_dma[q].add(prev)
                waits.append(prev)

        def fn(e, out=out, in_=in_, kw=kw):
            return e.dma_start(out=out, in_=in_, **kw)

        rec = dict(eng=q, fn=fn, waits=waits, dma=tok, inc=False, idx=None)
        self.ops.append(rec)
        self._commit(tok, r, w)
        return tok

    def barrier(self, full=False):
        if Prog.LIMIT is not None and len(self.ops) >= Prog.LIMIT:
            return
        toks = []
        for e in self.ENGS:
            if self.ncomp[e] > 0:
                toks.append(("c", e, self.ncomp[e] - 1))
        for q, n in self.ndma.items():
            for i in range(max(0, n - NDMASEM), n):
                if full or ("d", q, i) not in self.bg:
                    toks.append(("d", q, i))
        for e in self.ENGS:
            if e in ("sp", "pe") and not full:
                continue
            waits = []
            for t in toks:
                if t[0] == "c":
                    if t[1] == e and e == "pe":
                        continue
                    if self.known[e].get(t[1], -1) >= t[2]:
                        continue
                    self.known[e][t[1]] = t[2]
                    waits.append(t)
                else:
                    if t in self.known_dma[e]:
                        continue
                    self.known_dma[e].add(t)
                    waits.append(t)
            if waits:
                self.ops.append(dict(eng=e, fn=None, waits=waits, dma=None, inc=False, idx=None))
        def keep(k):
            return not full
        self.writers = {k: v for k, v in self.writers.items() if keep(k)}
        self.readers = {k: v for k, v in self.readers.items() if keep(k)}

    def emit(self, csem, dsem):
        for rec in self.ops:
            for t in rec["waits"]:
                if t[0] == "c":
                    self.comp_ops[t[1]][t[2]]["inc"] = True
        count = {}
        for e in self.ENGS:
            c = 0
            for rec in self.comp_ops[e]:
                if rec["inc"]:
                    c += 1
                rec["cnt"] = c
            count[e] = c
        for rec in self.ops:
            e = self.eng[rec["eng"]]
            for t in rec["waits"]:
                if t[0] == "c":
                    e.wait_ge(csem[t[1]], self.comp_ops[t[1]][t[2]]["cnt"])
                else:
                    _, q, n = t
                    e.wait_ge(dsem[q][n % NDMASEM], 16 * (n // NDMASEM + 1))
            if rec["fn"] is None:
                continue
            ins = rec["fn"](e)
            if rec["dma"] is not None:
                _, q, n = rec["dma"]
                ins.then_inc(dsem[q][n % NDMASEM], 16)
            elif rec["inc"]:
                ins.then_inc(csem[rec["eng"]], 1)
        sp = self.eng["sp"]
        for q, n in self.ndma.items():
            for i in range(max(0, n - NDMASEM), n):
                sp.wait_ge(dsem[q][i % NDMASEM], 16 * (i // NDMASEM + 1))
        return count


class Arena:
    DEBUG = False

    def __init__(self, ap, nwords):
        self.ap = ap
        self.n = nwords
        self.top = 0

    LOG = []

    def f32(self, n):
        off = self.top
        self.top += n
        assert self.top <= self.n, ("arena overflow", self.top, self.n)
        if Arena.DEBUG:
            import inspect
            fr = inspect.stack()
            ctxs = [f.code_context[0].strip() for f in fr[1:3] if f.code_context]
            Arena.LOG.append((off, n, " | ".join(ctxs)))
        return self.ap[:, off:off + n]

    def bf16(self, n):
        w = (n + 1) // 2
        return self.f32(w).bitcast(BF16)[:, 0:n]

    def mark(self):
        return self.top

    def release(self, m):
        self.top = m


def _inv_counts_1d(L, w):
    t = np.arange(L)
    lo = np.clip(t - w // 2, 0, L)
    hi = np.clip(t + (w - w // 2), 0, L)
    return 1.0 / (hi - lo).astype(np.float64)


def host_constants(nst):
    bf = ml_dtypes.bfloat16
    c = {}
    c["k_ident"] = np.eye(128, dtype=np.float32).astype(bf)
    c["k_ones"] = np.ones((128, 128), np.float32).astype(bf)
    s = np.arange(128)[:, None]
    t = np.arange(128)[None, :]
    same = (s // CH) == (t // CH)
    c["k_maskf"] = (same & (s <= t)).astype(np.float32).astype(bf)
    c["k_maskb"] = (same & (s >= t)).astype(np.float32).astype(bf)
    ind = np.full(8 * 65, 1e-30, np.float32)
    ind[::65] = 1.0
    c["k_ind"] = ind
    ip = np.zeros((4, 512), np.float32)
    for wi, w in enumerate((2, 4, 8, 16)):
        v = _inv_counts_1d(256, w)
        ip[wi] = np.concatenate([v, v]).astype(np.float32)
    c["k_invp"] = ip
    rows = max(nst, 1) * 8
    isx = np.zeros((4, rows * 64), np.float32)
    for wi, w in enumerate((2, 4, 8, 16)):
        ir = _inv_counts_1d(rows, w)
        ic = _inv_counts_1d(64, w)
        isx[wi] = (ir[:, None] * ic[None, :]).reshape(-1).astype(np.float32)
    c["k_invs"] = isx
    return c


def build_program(npt, nst, debug=False, stages=3):
    nc = bass.Bass("TRN2", target_bir_lowering=False)
    P = Prog(nc)
    NTOKS = nst * TT

    def din(name, shape, dt=F32):
        return nc.dram_tensor(name, list(shape), dt, kind="ExternalInput").ap()

    def dout(name, shape, dt=F32):
        return nc.dram_tensor(name, list(shape), dt, kind="ExternalOutput").ap()

    def dscr(name, shape, dt=F32):
        return nc.dram_tensor(name, list(shape), dt, kind="Internal").ap()

    xp = din("xp", [max(npt, 1) * TT, D])
    xs = din("xs", [max(NTOKS, TT), D])
    st = din("st", [2, HEADS, 128, 128])
    cvec = din("cvec", [2, D])
    w_ada = din("w_ada", [D, 6 * D])
    b_ada = din("b_ada", [6 * D])
    norm1_w = din("norm1_w", [D])
    w_in = din("w_in", [D, 6144])
    lb_param = din("lb_param", [2, 2, 1024])
    hg_norm_w = din("hg_norm_w", [128])
    w_pool = din("w_pool", [4, 256, 256])
    pool_scale = din("pool_scale", [1024])
    w_out = din("w_out", [D, D])
    norm2_w = din("norm2_w", [D])
    w_up = din("w_up", [D, 2 * DFF])
    conv_w = din("conv_w", [3, 2 * DFF])
    conv_b = din("conv_b", [2 * DFF])
    w_down = din("w_down", [DFF, D])
    final_norm_w = din("final_norm_w", [D])
    k_ident = din("k_ident", [128, 128], BF16)
    k_ones = din("k_ones", [128, 128], BF16)
    k_maskf = din("k_maskf", [128, 128], BF16)
    k_maskb = din("k_maskb", [128, 128], BF16)
    k_ind = din("k_ind", [520])
    k_invp = din("k_invp", [4, 512])
    k_invs = din("k_invs", [4, max(nst, 1) * 512])
    yp = dout("yp", [max(npt, 1) * TT, D])
    ys = dout("ys", [max(NTOKS, TT), D])
    ns = dout("ns", [max(npt, 1) * 2, 2, HEADS, 128, 128])
    wq_head = dscr("wq_head", [4, 2, 128, 2, 16, 256], BF16)
    wq_v = dscr("wq_v", [2, 128, 16, 512], BF16)
    wq_p = dscr("wq_p", [2, 128, 2, 16, 256], BF16)
    wq_pool = dscr("wq_pool", [128, 4, 2, 256], BF16)
    wq_out = dscr("wq_out", [4, 128, 16, 512], BF16)
    wq_up = dscr("wq_up", [22, 128, 2, 16, 256], BF16)
    wq_down = dscr("wq_down", [4, 4, 128, 11, 512], BF16)
    modscr = dscr("modscr", [2, 6 * D])
    pscr = dscr("pscr", [8, 128, max(NTOKS, TT)], BF16)
    sbscr = dscr("sbscr", [max(nst, 1), 128, HEADS, 128])
    x1def = dscr("x1def", [8, D])
    hscr = dscr("hscr", [max(nst, 1), 128, 16 * 512], BF16)
    bscr = dscr("bscr", [max(nst, 1), HEADS, 128, 1296])
    vscr = dscr("vscr", [max(nst, 1), 128, 4 * 1024], BF16)

    ARENA_WORDS = 53200
    import contextlib
    with contextlib.ExitStack() as es:
        arena_t = es.enter_context(nc.sbuf_tensor("arena", [128, ARENA_WORDS], F32))
        banks = [es.enter_context(nc.psum_tensor(f"psb{i}", [128, 512], F32)) for i in range(8)]
        csem = {e: es.enter_context(nc.semaphore(f"c_{e}")) for e in Prog.ENGS}
        dsem = {q: [es.enter_context(nc.semaphore(f"d_{q}{i}")) for i in range(NDMASEM)]
                for q in ("sp", "pool", "act")}
        A = Arena(arena_t[:], ARENA_WORDS)

        def bank(i):
            return banks[i][:]

        def bankb(i):
            return banks[i][:].bitcast(BF16)

        def pk(i, lo=0, hi=4):
            return [("ps", i)]

        ident = A.bf16(128)
        ones = A.bf16(128)
        maskf = A.bf16(128)
        maskb = A.bf16(128)
        ind = A.f32(520)
        epsc = A.f32(1)
        zerob = A.bf16(128)
        n1col = A.f32(16)
        n2col = A.f32(16)
        modc = A.f32(2 * 4 * 16).rearrange("p (c v k) -> p c v k", c=2, v=4)
        acol = A.f32(2 * 2 * 16).rearrange("p (c v k) -> p c v k", c=2, v=2)
        lbc = A.f32(16).rearrange("p (d h) -> p d h", d=2)
        omlc = A.f32(16).rearrange("p (d h) -> p d h", d=2)
        nomlc = A.f32(16).rearrange("p (d h) -> p d h", d=2)
        hgw = A.f32(1)
        pscol = A.f32(8)
        cwcol = A.f32(3 * 88).rearrange("p (j c) -> p j c", j=3)
        cbcol = A.f32(88)
        Sf = A.f32(HEADS * 128).rearrange("p (h v) -> p h v", h=HEADS)
        Sb = A.f32(HEADS * 128).rearrange("p (h v) -> p h v", h=HEADS)
        udef = A.f32(88 * 8 * 3).rearrange("p (c j e) -> p c j e", c=88, j=8)
        wslots = [A.bf16(8192) for _ in range(3)]
        wctr = [0]
        PERSIST_TOP = A.mark()
        LOW_WORDS = 14336

        sp_q = "sp"
        io_q = "pool"

        def load_w(src_ap, nelem, src_key, shape_str=None, **shape_kw):
            i = wctr[0] % 3
            wctr[0] += 1
            dst = wslots[i][:, 0:nelem]
            P.dma(sp_q, dst, src_ap, r=[src_key], w=[("wslot", i)])
            v = dst
            if shape_str is not None:
                v = dst.rearrange(shape_str, **shape_kw)
            return v, ("wslot", i)

        def ld(dst, src, key, q=io_q, **kw):
            P.dma(q, dst, src, r=[], w=[key], **kw)

        ld(ident, k_ident, "ident")
        ld(ones, k_ones, "ones")
        ld(maskf, k_maskf, "maskf")
        ld(maskb, k_maskb, "maskb")
        ld(ind, k_ind.partition_broadcast(128), "ind")
        P.op("dve", lambda e: e.memset(epsc, EPS), w=["epsc"])
        P.op("dve", lambda e: e.memset(zerob, 0.0), w=["zerob"])
        P.op("dve", lambda e: e.memset(udef.rearrange("p c j e -> p (c j e)"), 0.0), w=["udef"])
        with nc.allow_non_contiguous_dma(reason="tiny per-feature parameter columns"):
            ld(n1col, norm1_w.rearrange("(k p) -> p k", p=128), "n1col", allow_slow_non_contiguous=True)
            ld(n2col, norm2_w.rearrange("(k p) -> p k", p=128), "n2col", allow_slow_non_contiguous=True)
            ld(hgw, hg_norm_w.rearrange("(p o) -> p o", o=1), "hgw", allow_slow_non_contiguous=True)
            ld(pscol, pool_scale.rearrange("(k p) -> p k", p=128), "pscol", allow_slow_non_contiguous=True)
            for j in range(3):
                ld(cwcol[:, j, :], conv_w[j].rearrange("(k p) -> p k", p=128), ("cwcol", j), allow_slow_non_contiguous=True)
            ld(cbcol, conv_b.rearrange("(k p) -> p k", p=128), "cbcol", allow_slow_non_contiguous=True)
            m0 = A.mark()
            lp = A.f32(32).rearrange("p (d l h) -> p d l h", d=2, l=2)
            for d_ in range(2):
                for l_ in range(2):
                    ld(lp[:, d_, l_, :], lb_param[d_, l_].rearrange("(h p) -> p h", p=128), ("lp", d_, l_), allow_slow_non_contiguous=True)
        lpk = [("lp", a, b) for a in range(2) for b in range(2)]
        P.op("dve", lambda e: e.tensor_tensor(out=lbc, in0=lp[:, :, 0, :], in1=lp[:, :, 1, :], op=ALU.subtract),
             r=lpk, w=["lbc"])
        P.op("act", lambda e: e.activation(out=lbc, in_=lbc, func=AF.Sigmoid), r=["lbc"], w=["lbc"])
        P.op("dve", lambda e: e.tensor_scalar(out=omlc, in0=lbc, scalar1=-1.0, scalar2=1.0, op0=ALU.mult, op1=ALU.add),
             r=["lbc"], w=["omlc"])
        P.op("dve", lambda e: e.tensor_scalar(out=nomlc, in0=omlc, scalar1=-1.0, scalar2=None, op0=ALU.mult),
             r=["omlc"], w=["nomlc"])

        conv_list = []

        def cvt(dst, src, key):
            conv_list.append((dst, src, key))

        def issue_conv(n):
            for _ in range(min(n, len(conv_list))):
                dst, src, key = conv_list.pop(0)
                P.dma("pool", dst, src, r=[], w=[key], bg=True)

        def statsrc(w_ap, r0, nrows, c0, ncols):
            return w_ap[r0:r0 + nrows, c0:c0 + ncols].rearrange("(kc p) c -> p kc c", p=128)

        for cg in range(2):
            cvt(wq_v[cg], statsrc(w_in, 0, D, 3072 + cg * 512, 512), ("wq_v", cg))
        hbase = [0, 1024, 2048, 4096]
        for hp in range(4):
            for t_ in range(2):
                cvt(wq_head[hp, 1, :, t_], statsrc(w_in, 0, D, hbase[2 + t_] + hp * 256, 256), ("wq_head", hp, 1))
        for s_ in range(2):
            for t_ in range(2):
                cvt(wq_p[s_, :, t_], statsrc(w_in, 0, D, 5120 + (s_ * 2 + t_) * 256, 256), ("wq_p", s_))
        N_CONV_FIRST = len(conv_list)
        for hp in range(4):
            for t_ in range(2):
                cvt(wq_head[hp, 0, :, t_], statsrc(w_in, 0, D, hbase[t_] + hp * 256, 256), ("wq_head", hp, 0))
        for gi in range(4):
            cvt(wq_pool[:, gi], w_pool[gi].rearrange("(kc p) c -> p kc c", p=128), "wq_pool_all")
        for cg in range(4):
            cvt(wq_out[cg], statsrc(w_out, 0, D, cg * 512, 512), ("wq_out", cg))
        for s_ in range(22):
            cvt(wq_up[s_, :, 0], statsrc(w_up, 0, D, s_ * 256, 256), ("wq_up_all", s_))
            cvt(wq_up[s_, :, 1], statsrc(w_up, 0, D, DFF + s_ * 256, 256), ("wq_up_all", s_))
        for cg in range(4):
            for part in range(4):
                cvt(wq_down[cg, part], statsrc(w_down, part * 1408, 1408, cg * 512, 512), ("wq_down", cg, part))

        m1 = A.mark()
        ccol = A.f32(32).rearrange("p (c k) -> p c k", c=2)
        csil = A.f32(32).rearrange("p (c k) -> p c k", c=2)
        rep = A.bf16(2 * 16 * 128).rearrange("p (c k m) -> p c k m", c=2, k=16)
        brow = [A.f32(512) for _ in range(2)]
        mrow = [A.f32(512) for _ in range(4)]
        with nc.allow_non_contiguous_dma(reason="conditioning vector as columns"):
            for c_ in range(2):
                ld(ccol[:, c_, :], cvec[c_].rearrange("(k p) -> p k", p=128), ("ccol", c_), allow_slow_non_contiguous=True)
        P.op("act", lambda e: e.activation(out=csil, in_=ccol, func=AF.Silu), r=[("ccol", 0), ("ccol", 1)], w=["csil"])
        for c_ in range(2):
            for k_ in range(16):
                P.op("dve", lambda e, c_=c_, k_=k_: e.tensor_scalar(
                    out=rep[:, c_, k_, :], in0=ones, scalar1=csil[:, c_, k_:k_ + 1], scalar2=None, op0=ALU.mult),
                    r=["csil", "ones"], w=[("rep", c_)])
        wa_slots = [A.bf16(8192).rearrange("p (k c) -> p k c", k=16) for _ in range(2)]
        for cg in range(24):
            wi = cg % 2
            wa = wa_slots[wi]
            P.dma("pool", wa, w_ada[:, cg * 512:(cg + 1) * 512].rearrange("(kc p) c -> p kc c", p=128),
                  r=[], w=[("wa", wi)])
            P.dma("sp", brow[cg % 2], b_ada[cg * 512:(cg + 1) * 512].partition_broadcast(128),
                  r=[], w=[("brow", cg % 2)])
            for c_ in range(2):
                bi = (cg % 2) * 2 + c_
                for k_ in range(16):
                    P.op("pe", lambda e, bi=bi, c_=c_, k_=k_, wa=wa: e.matmul(
                        bank(bi), lhsT=rep[:, c_, k_, :], rhs=wa[:, k_, :], start=(k_ == 0), stop=(k_ == 15)),
                        r=[("rep", c_), ("wa", wi)], w=pk(bi))
                mr = mrow[bi]
                P.op("dve", lambda e, bi=bi, mr=mr, cg=cg: e.tensor_tensor(
                    out=mr, in0=bank(bi), in1=brow[cg % 2], op=ALU.add),
                    r=pk(bi) + [("brow", cg % 2)], w=[("mrow", bi)])
                P.dma("sp", modscr[c_:c_ + 1, cg * 512:(cg + 1) * 512], mr[0:1, :], r=[("mrow", bi)], w=[("modscr", c_, cg)])
        allmod = [("modscr", c_, cg) for c_ in range(2) for cg in range(24)]
        with nc.allow_non_contiguous_dma(reason="modulation vectors as columns"):
            for c_ in range(2):
                for vi, v_ in enumerate((0, 1, 3, 4)):
                    P.dma("sp", modc[:, c_, vi, :], modscr[c_, v_ * D:(v_ + 1) * D].rearrange("(k p) -> p k", p=128),
                          r=allmod, w=[("modc", c_, vi)], allow_slow_non_contiguous=True)
        for c_ in range(2):
            for ni, (vi, ncol, nk) in enumerate(((1, n1col, "n1col"), (3, n2col, "n2col"))):
                P.op("dve", lambda e, c_=c_, ni=ni, vi=vi: e.tensor_scalar(
                    out=acol[:, c_, ni, :], in0=modc[:, c_, vi, :], scalar1=1.0, scalar2=None, op0=ALU.add),
                    r=[("modc", c_, vi)], w=[("acol", c_, ni)])
                P.op("dve", lambda e, c_=c_, ni=ni, ncol=ncol: e.tensor_tensor(
                    out=acol[:, c_, ni, :], in0=acol[:, c_, ni, :], in1=ncol, op=ALU.mult),
                    r=[("acol", c_, ni), nk], w=[("acol", c_, ni)])
        P.barrier(full=True)
        A.release(m0)
        assert A.mark() == PERSIST_TOP
        issue_conv(N_CONV_FIRST)
        n_rest = len(conv_list)
        per_tile = (n_rest + max(nst, 1) - 1) // max(nst, 1)

        def norm_phase(get_src, ctx, ni, hbuf, hname, xn, ss, rt, rstd, junk):
            a_c = acol[:, ctx, ni, :]
            b_c = modc[:, ctx, 0 if ni == 0 else 2, :]
            srcs = {0: get_src(0), 1: get_src(1)}
            for tb in range(4):
                src, skeys = srcs[tb]
                P.op("act", lambda e, src=src, tb=tb: e.activation(
                    out=junk, in_=src, func=AF.Square, accum_out=ss[:, tb:tb + 1]), r=skeys, w=[("ss", tb)])
                P.op("act", lambda e, tb=tb: e.activation(
                    out=rt[:, tb:tb + 1], in_=ss[:, tb:tb + 1], func=AF.Sqrt, scale=1.0 / D, bias=epsc),
                    r=[("ss", tb), "epsc"], w=[("rt", tb)])
                P.op("dve", lambda e, tb=tb: e.reciprocal(out=rstd[:, tb:tb + 1], in_=rt[:, tb:tb + 1]),
                     r=[("rt", tb)], w=[("rstd", tb)])
                xb = xn[tb % 2]
                P.op("dve", lambda e, src=src, xb=xb, tb=tb: e.tensor_scalar(
                    out=xb, in0=src, scalar1=rstd[:, tb:tb + 1], scalar2=None, op0=ALU.mult),
                    r=skeys + [("rstd", tb)], w=[("xn", tb % 2)])
                if tb + 2 < 4:
                    srcs[tb + 2] = get_src(tb + 2)
                for g_ in range(2):
                    bi = (tb % 2) * 2 + g_
                    pb = bankb(bi).rearrange("p (k t) -> p k t", k=8)
                    for kk in range(8):
                        kc = g_ * 8 + kk
                        P.op("pe", lambda e, pb=pb, kk=kk, kc=kc, xb=xb: e.transpose(
                            out=pb[:, kk, :], in_=xb[:, kc * 128:(kc + 1) * 128], identity=ident),
                            r=[("xn", tb % 2), "ident"], w=pk(bi))
                    for kk in range(8):
                        kc = g_ * 8 + kk
                        if True:
                            P.op("dve", lambda e, pb=pb, kk=kk, kc=kc, tb=tb: e.tensor_scalar(
                                out=hbuf[:, kc, tb * 128:(tb + 1) * 128], in0=pb[:, kk, :],
                                scalar1=a_c[:, kc:kc + 1], scalar2=b_c[:, kc:kc + 1], op0=ALU.mult, op1=ALU.add),
                                r=pk(bi) + [("acol", ctx, ni), ("modc", ctx, 0 if ni == 0 else 2)],
                                w=[(hname, kc, tb)])
                        else:
                            P.op("act", lambda e, pb=pb, kk=kk, kc=kc, tb=tb: e.activation(
                                out=hbuf[:, kc, tb * 128:(tb + 1) * 128], in_=pb[:, kk, :], func=AF.Identity,
                                scale=a_c[:, kc:kc + 1], bias=b_c[:, kc:kc + 1]),
                                r=pk(bi) + [("acol", ctx, ni), ("modc", ctx, 0 if ni == 0 else 2)],
                                w=[(hname, kc, tb)])

        def hkeys(hname, kc=None):
            if kc is None:
                return [(hname, k, t) for k in range(16) for t in range(4)]
            return [(hname, kc, t) for t in range(4)]

        def proj_fm(bi, wv, wkey, hbuf, hname):
            for kc in range(16):
                P.op("pe", lambda e, kc=kc: e.matmul(bank(bi), lhsT=wv[:, kc, :], rhs=hbuf[:, kc, :],
                                                      start=(kc == 0), stop=(kc == 15)),
                     r=[wkey] + hkeys(hname, kc), w=pk(bi))

        def proj_tm(bset, wv, wkey, nk, kc0, act_ap, akeyfn, first, last):
            for kl in range(nk):
                kc = kc0 + kl
                for tb in range(4):
                    P.op("pe", lambda e, kl=kl, kc=kc, tb=tb: e.matmul(
                        bank(bset[tb]), lhsT=act_ap[:, kc, tb * 128:(tb + 1) * 128], rhs=wv[:, kl, :],
                        start=(first and kl == 0), stop=(last and kl == nk - 1)),
                        r=[wkey] + akeyfn(kc, tb), w=pk(bset[tb]))

        def mixer(kind, j, ctx, xsrc, prepass):
            P.barrier()
            mk = A.mark()
            assert mk == PERSIST_TOP
            xblk = [A.f32(D) for _ in range(2)]
            junk = A.bf16(D)
            xn = [A.bf16(D) for _ in range(2)]
            ss = A.f32(4); rt = A.f32(4); rstd = A.f32(4)
            invt = A.f32(512)
            assert A.mark() <= mk + 8192
            A.top = mk + 8192
            vtok = A.bf16(4 * 1024).rearrange("p (b c) -> p b c", b=4)
            h = A.bf16(16 * 512).rearrange("p (k t) -> p k t", k=16)
            assert A.mark() == mk + LOW_WORDS
            A.top = mk + LOW_WORDS
            omix = A.bf16(16 * 512).rearrange("p (k t) -> p k t", k=16)
            do_heads = True
            do_s3 = not (prepass and j == 0)
            if kind == "s" and do_s3:
                if j == nst - 1:
                    P.dma(sp_q, Sb, st[1].rearrange("h d v -> d h v"), r=[], w=["Sb"])
                elif not prepass:
                    P.dma(sp_q, Sb, sbscr[j], r=[("sbscr", j)], w=["Sb"])
                if (not prepass) and j == 0:
                    P.dma(sp_q, Sf, st[0].rearrange("h d v -> d h v"), r=[], w=["Sf"])

            X1KEYS = [("x1", tb_, cg_) for tb_ in range(4) for cg_ in range(4)]

            def get_src(tb):
                xb = xblk[tb % 2]
                P.dma(io_q, xb, xsrc[j * TT + tb * 128: j * TT + (tb + 1) * 128, :], r=[],
                      w=[("xblk", tb % 2)] + (X1KEYS if tb < 2 else []))
                return xb, [("xblk", tb % 2)]
            reuse = (kind == "s") and (not prepass)
            vkeys_all = [("vtok", tb, cg) for tb in range(4) for cg in range(2)]
            if reuse:
                P.dma(sp_q, h.rearrange("p k t -> p (k t)"), hscr[j], r=[("hscr", j)], w=hkeys("h") + hkeys("h2"))
                P.dma(io_q, vtok.rearrange("p b c -> p (b c)"), vscr[j], r=[("vscr", j)], w=vkeys_all)
            else:
                norm_phase(get_src, ctx, 0, h, "h", xn, ss, rt, rstd, junk)
                if prepass:
                    issue_conv(per_tile)
                    P.dma(io_q, hscr[j], h.rearrange("p k t -> p (k t)"), r=hkeys("h"), w=[("hscr", j)])

            def vproj():
                for cg in range(2):
                    wv, wkey = load_w(wq_v[cg].rearrange("p k c -> p (k c)"), 8192, ("wq_v", cg),
                                      "p (k c) -> p k c", k=16)
                    bset = [4 + t for t in range(4)] if prepass else [cg * 4 + t for t in range(4)]
                    proj_tm(bset, wv, wkey, 16, 0, h, lambda kc, tb: [("h", kc, tb)], True, True)
                    for tb in range(4):
                        P.op("act", lambda e, tb=tb, cg=cg, bset=bset: e.activation(
                            out=vtok[:, tb, cg * 512:(cg + 1) * 512], in_=bank(bset[tb]), func=AF.Copy),
                            r=pk(bset[tb]), w=[("vtok", tb, cg)])
                if prepass:
                    P.dma(io_q, vscr[j], vtok.rearrange("p b c -> p (b c)"), r=vkeys_all, w=[("vscr", j)])

            def pproj_prepass():
                pbuf = omix[:, 0:8, :]
                for s_ in range(2):
                    wv, wkey = load_w(wq_p[s_].rearrange("p t k c -> p (t k c)"), 8192, ("wq_p", s_),
                                      "p (t k c) -> p t k c", t=2, k=16)
                    for t_ in range(2):
                        for hh in range(2):
                            pc = s_ * 4 + t_ * 2 + hh
                            bi = 5 + pc % 3
                            proj_fm(bi, wv[:, t_, :, hh * 128:(hh + 1) * 128], wkey, h, "h")
                            P.op("act", lambda e, pc=pc, bi=bi: e.activation(out=pbuf[:, pc, :], in_=bank(bi), func=AF.Copy),
                                 r=pk(bi), w=[("pbuf", pc)])
                for pc in range(8):
                    P.dma(io_q, pscr[pc, :, j * TT:(j + 1) * TT], pbuf[:, pc, :], r=[("pbuf", pc)], w=[("pscr", pc, j)])
            if not reuse and not prepass:
                vproj()
            dm = None
            if not prepass:
                PA = Arena(arena_t[:], ARENA_WORDS)
                PA.top = mk
                alias = [("xblk", 0), ("xblk", 1), ("xn", 0), ("xn", 1)] + X1KEYS
                dm = PA.bf16(8 * 512).rearrange("p (k t) -> p k t", k=8)
                SH = [(1, 0), (1, 1), (2, 2), (4, 4)]

                def need_ranges(nlev, lo, hi):
                    rr = [None] * nlev
                    cur = (lo, hi)
                    for k_ in range(nlev - 1, -1, -1):
                        rr[k_] = cur
                        cur = (cur[0] - SH[k_][0], cur[1] + SH[k_][1])
                    return rr
                if kind == "p":
                    pbuf = PA.bf16(8 * 512).rearrange("p (k t) -> p k t", k=8)
                    CA = PA.f32(2 * 272).rearrange("p (r w) -> p r w", r=2)
                    CB = PA.f32(2 * 272).rearrange("p (r w) -> p r w", r=2)
                    nC, Rn = 256, 2
                else:
                    RA = PA.bf16(24 * 64).rearrange("p (r w) -> p r w", r=24)
                    RB = PA.f32(24 * 64).rearrange("p (r w) -> p r w", r=24)
                    RC = PA.f32(24 * 64).rearrange("p (r w) -> p r w", r=24)
                    CA = PA.f32(8 * 80).rearrange("p (r w) -> p r w", r=8)
                    CB = PA.f32(8 * 80).rearrange("p (r w) -> p r w", r=8)
                    nC, Rn = 64, 8
                assert PA.top <= mk + 7168
                P.op("pool", lambda e: e.memset(CA.rearrange("p r w -> p (r w)"), 0.0), w=["CA"] + alias)
                P.op("pool", lambda e: e.memset(CB.rearrange("p r w -> p (r w)"), 0.0), w=["CB"])
                if kind == "p":
                    for s_ in range(2):
                        wv, wkey = load_w(wq_p[s_].rearrange("p t k c -> p (t k c)"), 8192, ("wq_p", s_),
                                          "p (t k c) -> p t k c", t=2, k=16)
                        for t_ in range(2):
                            for hh in range(2):
                                pc = s_ * 4 + t_ * 2 + hh
                                bi = pc % 8
                                proj_fm(bi, wv[:, t_, :, hh * 128:(hh + 1) * 128], wkey, h, "h")
                                P.op("act", lambda e, pc=pc, bi=bi: e.activation(out=pbuf[:, pc, :], in_=bank(bi), func=AF.Copy),
                                     r=pk(bi), w=[("pbuf", pc)] + (alias if pc == 0 else []))
                else:
                    g_lo = max(0, 8 * j - 8)
                    g_hi = min(nst * 8, 8 * j + 16)
                    r_lo = g_lo - (8 * j - 8)
                    r_hi = r_lo + (g_hi - g_lo)
                    P.op("pool", lambda e: e.memset(RA.rearrange("p r w -> p (r w)"), 0.0), w=["RA"])
                for pc in range(8):
                    gi = pc // 2
                    nlev = gi + 1
                    if pc % 2 == 0:
                        src_iv = k_invp[gi] if kind == "p" else k_invs[gi, j * TT:(j + 1) * TT]
                        P.dma(io_q, invt, src_iv.partition_broadcast(128), r=[], w=["invt"])
                    if kind == "s":
                        P.dma(io_q, RA[:, r_lo:r_hi, :].rearrange("p r w -> p (r w)"), pscr[pc, :, g_lo * 64:g_hi * 64],
                              r=[], w=["RA"])
                        rr = need_ranges(nlev, 8, 16)
                        srcb, skey = RA, "RA"
                        for lv in range(nlev):
                            a_, b_ = SH[lv]
                            lo, hi = rr[lv]
                            dstb, dkey = (RB, "RB") if skey != "RB" else (RC, "RC")
                            P.op("pool", lambda e, srcb=srcb, dstb=dstb, a_=a_, b_=b_, lo=lo, hi=hi: e.tensor_tensor(
                                out=dstb[:, lo:hi, :], in0=srcb[:, lo - a_:hi - a_, :], in1=srcb[:, lo + b_:hi + b_, :],
                                op=ALU.add), r=[skey], w=[dkey])
                            srcb, skey = dstb, dkey
                        P.op("pool", lambda e, srcb=srcb: e.tensor_copy(out=CA[:, :, 8:72], in_=srcb[:, 8:16, :]),
                             r=[skey], w=["CA"])
                        rawp = RA[:, 8:16, :]
                        rawk = "RA"
                    else:
                        P.op("pool", lambda e, pc=pc: e.tensor_copy(
                            out=CA[:, :, 8:264], in_=pbuf[:, pc, :].rearrange("p (r t) -> p r t", r=2)),
                            r=[("pbuf", pc)], w=["CA"])
                        rawp = pbuf[:, pc, :].rearrange("p (r t) -> p r t", r=2)
                        rawk = ("pbuf", pc)
                    rr = need_ranges(nlev, 8, 8 + nC)
                    srcb, skey = CA, "CA"
                    for lv in range(nlev):
                        a_, b_ = SH[lv]
                        lo, hi = rr[lv]
                        dstb, dkey = (CB, "CB") if skey == "CA" else (CA, "CA")
                        P.op("pool", lambda e, srcb=srcb, dstb=dstb, a_=a_, b_=b_, lo=lo, hi=hi: e.tensor_tensor(
                            out=dstb[:, :, lo:hi], in0=srcb[:, :, lo - a_:hi - a_], in1=srcb[:, :, lo + b_:hi + b_],
                            op=ALU.add), r=[skey], w=[dkey])
                        srcb, skey = dstb, dkey
                    fin = srcb[:, :, 8:8 + nC]
                    iv = invt.rearrange("p (r t) -> p r t", r=Rn)
                    dmo = dm[:, pc, :].rearrange("p (r t) -> p r t", r=Rn)
                    P.op("pool", lambda e, fin=fin, iv=iv: e.tensor_tensor(out=fin, in0=fin, in1=iv, op=ALU.mult),
                         r=[skey, "invt"], w=[skey])
                    P.op("pool", lambda e, fin=fin, rawp=rawp, dmo=dmo: e.tensor_tensor(out=dmo, in0=fin, in1=rawp, op=ALU.subtract),
                         r=[skey, rawk], w=[("dm", pc)])
                    for bb, bk in ((CA, "CA"), (CB, "CB")):
                        P.op("pool", lambda e, bb=bb: e.memset(bb[:, :, 0:8], 0.0), r=[bk], w=[bk])
                        P.op("pool", lambda e, bb=bb: e.memset(bb[:, :, 8 + nC:16 + nC], 0.0), r=[bk], w=[bk])
            hm = A.mark()
            qs = [A.f32(512) for _ in range(2)]
            gs = [A.f32(512) for _ in range(2)]
            reuse = (kind == "s") and (not prepass)
            use_rec = (kind == "s")
            e1 = not reuse
            sg = [[A.f32(512), A.f32(512) if e1 else None] for _ in range(2)]
            fbuf = [A.f32(520), A.f32(520) if e1 else None]
            kbuf = [A.f32(512), A.f32(512) if e1 else None]
            Rb = [A.f32(520), A.f32(520) if (e1 and not prepass) else None]
            Pb = [[A.f32(520), None if use_rec else A.f32(520)] for _ in range(2)]
            qT = [[A.bf16(512) for _ in range(2)] for _ in range(2)]
            kT = [[A.bf16(512), None if use_rec else A.bf16(512)] for _ in range(2)]
            Rbb = [None, None]
            if use_rec:
                brec = [A.f32(1296) for _ in range(2)]
                for g_ in range(2):
                    kT[g_][1] = brec[g_][:, 0:256].bitcast(BF16)
                    Pb[g_][1] = brec[g_][:, 256:776]
                    Rbb[g_] = brec[g_][:, 776:1296]
            ktok = [A.bf16(2 * 4 * 128).rearrange("p (d b c) -> p d b c", d=2, b=4) for _ in range(2)]
            Sst = [A.bf16(8 * 128).rearrange("p (c v) -> p c v", c=8) for _ in range(2)]
            Amat = [A.bf16(4 * 128).rearrange("p (b t) -> p b t", b=4) for _ in range(2)]
            Ybuf = [[A.f32(128) for _ in range(2)] for _ in range(2)]
            Sout = [A.f32(128) for _ in range(4)]
            osq = A.bf16(512)
            ort = A.f32(512)
            onb = A.f32(512)
            for d_ in range(2):
                if fbuf[d_] is not None:
                    P.op("dve", lambda e, d_=d_: e.memset(fbuf[d_], 0.0), w=[("fbuf", d_)])
                P.op("dve", lambda e, d_=d_: e.memset(ktok[d_].rearrange("p d b c -> p (d b c)"), 0.0),
                     w=[("ktok", d_, 0), ("ktok", d_, 1)])
            nseg = 2 if kind == "p" else 1
            cps = 8 // nseg
            dirs = (1,) if prepass else (0, 1)
            edirs = (1,) if prepass else ((0,) if reuse else (0, 1))
            wkeep = {}

            def S1(hd):
                g = hd % 2
                hp, hh = hd // 2, hd % 2
                if prepass:
                    if hh == 0:
                        wkeep[1] = load_w(wq_head[hp, 1].rearrange("p t k c -> p (t k c)"), 8192,
                                          ("wq_head", hp, 1), "p (t k c) -> p t k c", t=2, k=16)
                    wv1, wkey1 = wkeep[1]
                    proj_fm(2, wv1[:, 0, :, hh * 128:(hh + 1) * 128], wkey1, h, "h")
                else:
                    if hh == 0:
                        wkeep[0] = load_w(wq_head[hp, 0].rearrange("p t k c -> p (t k c)"), 8192,
                                          ("wq_head", hp, 0), "p (t k c) -> p t k c", t=2, k=16)
                        wkeep[1] = load_w(wq_head[hp, 1].rearrange("p t k c -> p (t k c)"), 8192,
                                          ("wq_head", hp, 1), "p (t k c) -> p t k c", t=2, k=16)
                    wv0, wkey0 = wkeep[0]
                    wv1, wkey1 = wkeep[1]
                    proj_fm(0, wv0[:, 0, :, hh * 128:(hh + 1) * 128], wkey0, h, "h")
                    proj_fm(1, wv0[:, 1, :, hh * 128:(hh + 1) * 128], wkey0, h, "h")
                    if not reuse:
                        proj_fm(2, wv1[:, 0, :, hh * 128:(hh + 1) * 128], wkey1, h, "h")
                    else:
                        P.dma(sp_q, brec[g], bscr[j, hd], r=[("bscr", j, hd)], w=[("kT", g, 1), ("Pb", g, 1), ("Rbb", g)])
                    proj_fm(3, wv1[:, 1, :, hh * 128:(hh + 1) * 128], wkey1, h, "h")
                    P.op("act", lambda e, g=g: e.activation(out=qs[g], in_=bank(0), func=AF.Silu), r=pk(0), w=[("qs", g)])
                for d_ in edirs:
                    zb_ = 1 + d_
                    P.op("act", lambda e, d_=d_, zb_=zb_, g=g: e.activation(out=sg[g][d_], in_=bank(zb_), func=AF.Sigmoid),
                         r=pk(zb_), w=[("sg", g, d_)])
                if not prepass:
                    P.op("act", lambda e, g=g: e.activation(out=gs[g], in_=bank(3), func=AF.Silu), r=pk(3), w=[("gs", g)])

            def S2(hd):
                g = hd % 2
                if reuse:
                    r3b = Rbb[g].rearrange("p (c t) -> p c t", c=8)
                    q3b = qs[g].rearrange("p (c t) -> p c t", c=8)
                    qT3b = qT[g][1].rearrange("p (c t) -> p c t", c=8)
                    P.op("dve", lambda e, q3b=q3b, r3b=r3b, qT3b=qT3b: e.tensor_tensor(
                        out=qT3b, in0=q3b, in1=r3b[:, :, 0:64], op=ALU.mult), r=[("qs", g), ("Rbb", g)], w=[("qT", g, 1)])
                for d_ in edirs:
                    f3 = fbuf[d_].rearrange("p (c t) -> p c t", c=8)
                    s3 = sg[g][d_].rearrange("p (c t) -> p c t", c=8)
                    P.op("act", lambda e, d_=d_, f3=f3, s3=s3, hd=hd: e.activation(
                        out=f3[:, :, 1:65], in_=s3, func=AF.Identity, scale=omlc[:, d_, hd:hd + 1], bias=lbc[:, d_, hd:hd + 1]),
                        r=[("sg", g, d_), "omlc", "lbc"], w=[("fbuf", d_)])
                    P.op("act", lambda e, d_=d_, hd=hd, g=g: e.activation(
                        out=kbuf[d_], in_=sg[g][d_], func=AF.Identity, scale=nomlc[:, d_, hd:hd + 1], bias=omlc[:, d_, hd:hd + 1]),
                        r=[("sg", g, d_), "omlc", "nomlc"], w=[("kbuf", d_)])
                    P.op("dve", lambda e, d_=d_, g=g: e.tensor_tensor_scan(
                        out=Pb[g][d_], data0=fbuf[d_], data1=ind, initial=0.0, op0=ALU.mult, op1=ALU.max),
                        r=[("fbuf", d_), "ind"], w=[("Pb", g, d_)])
                    p3 = Pb[g][d_].rearrange("p (c t) -> p c t", c=8)
                    k3 = kbuf[d_].rearrange("p (c t) -> p c t", c=8)
                    kT3 = kT[g][d_].rearrange("p (c t) -> p c t", c=8)
                    if not prepass:
                        P.op("dve", lambda e, d_=d_, g=g: e.reciprocal(out=Rb[d_], in_=Pb[g][d_]), r=[("Pb", g, d_)], w=[("Rb", d_)])
                        r3 = Rb[d_].rearrange("p (c t) -> p c t", c=8)
                        q3 = qs[g].rearrange("p (c t) -> p c t", c=8)
                        qT3 = qT[g][d_].rearrange("p (c t) -> p c t", c=8)
                    if d_ == 0:
                        P.op("dve", lambda e, q3=q3, p3=p3, qT3=qT3: e.tensor_tensor(
                            out=qT3, in0=q3, in1=p3[:, :, 1:65], op=ALU.mult), r=[("qs", g), ("Pb", g, 0)], w=[("qT", g, 0)])
                        P.op("dve", lambda e, k3=k3, r3=r3, kT3=kT3: e.tensor_tensor(
                            out=kT3, in0=k3, in1=r3[:, :, 1:65], op=ALU.mult), r=[("kbuf", 0), ("Rb", 0)], w=[("kT", g, 0)])
                    else:
                        if not prepass:
                            P.op("dve", lambda e, q3=q3, r3=r3, qT3=qT3: e.tensor_tensor(
                                out=qT3, in0=q3, in1=r3[:, :, 0:64], op=ALU.mult), r=[("qs", g), ("Rb", 1)], w=[("qT", g, 1)])
                        P.op("dve", lambda e, k3=k3, p3=p3, kT3=kT3: e.tensor_tensor(
                            out=kT3, in0=k3, in1=p3[:, :, 0:64], op=ALU.mult), r=[("kbuf", 1), ("Pb", g, 1)], w=[("kT", g, 1)])
                        if prepass:
                            P.op("dve", lambda e, g=g: e.reciprocal(out=Rbb[g], in_=Pb[g][1]), r=[("Pb", g, 1)], w=[("Rbb", g)])
                            P.dma(io_q, bscr[j, hd], brec[g], r=[("kT", g, 1), ("Pb", g, 1), ("Rbb", g)], w=[("bscr", j, hd)])

            def S3(hd):
                g = hd % 2
                pkb = bankb(4).rearrange("p (d b c) -> p d b c", d=2, b=4)
                for d_ in dirs:
                    for tb in range(4):
                        P.op("pe", lambda e, d_=d_, tb=tb, g=g: e.transpose(
                            out=pkb[:, d_, tb, :], in_=kT[g][d_][:, tb * 128:(tb + 1) * 128], identity=ident),
                            r=[("kT", g, d_), "ident"], w=pk(4))
                for d_ in dirs:
                    for half in range(2):
                        hs_ = slice(half * 64, (half + 1) * 64)
                        P.op("act", lambda e, d_=d_, half=half, hs_=hs_: e.activation(
                            out=ktok[half][hs_, d_], in_=pkb[hs_, d_], func=AF.Copy),
                            r=pk(4), w=[("ktok", half, d_)])
                for d_ in dirs:
                    for c in range(8):
                        tb, half = c // 2, c % 2
                        bi = d_ * 2 + c // 4
                        ub = bank(bi).rearrange("p (c v) -> p c v", c=4)
                        P.op("pe", lambda e, d_=d_, tb=tb, half=half, ub=ub, c=c, hd=hd: e.matmul(
                            ub[:, c % 4, :], lhsT=ktok[half][:, d_, tb, :],
                            rhs=vtok[:, tb, hd * 128:(hd + 1) * 128], start=True, stop=True),
                            r=[("ktok", half, d_), ("vtok", tb, hd // 4)], w=pk(bi))
                if not prepass:
                    for d_ in range(2):
                        sb_ = bank(5 + d_).rearrange("p (b t) -> p b t", b=4)
                        for tb in range(4):
                            P.op("pe", lambda e, d_=d_, tb=tb, sb_=sb_, g=g: e.matmul(
                                sb_[:, tb, :], lhsT=kT[g][d_][:, tb * 128:(tb + 1) * 128],
                                rhs=qT[g][d_][:, tb * 128:(tb + 1) * 128],
                                start=True, stop=True), r=[("kT", g, d_), ("qT", g, d_)], w=pk(5 + d_))
                chains = {0: [], 1: []}
                for d_ in dirs:
                    def Q(eng, fn, r=(), w=(), d_=d_):
                        chains[d_].append(lambda: P.op(eng, fn, r=r, w=w))

                    def Qd(q, out, in_, r=(), w=(), d_=d_):
                        chains[d_].append(lambda: P.dma(q, out, in_, r=r, w=w))
                    p3 = Pb[g][d_].rearrange("p (c t) -> p c t", c=8)
                    pbk = ("Pb", g, d_)

                    def U(c, d_=d_):
                        bi = d_ * 2 + c // 4
                        return bank(bi).rearrange("p (c v) -> p c v", c=4)[:, c % 4, :], ("ps", bi)

                    def Dc(c, p3=p3):
                        return p3[:, c, 64:65]
                    for sgm in range(nseg):
                        c0 = sgm * cps
                        if d_ == 0:
                            if kind == "p":
                                Q("act", lambda e, c0=c0: e.activation(out=Sst[0][:, c0, :], in_=zerob, func=AF.Copy),
                                     r=["zerob"], w=[("Sst", 0, c0)])
                                u, uk = U(c0)
                                Q("dve", lambda e, u=u: e.tensor_copy(out=Ybuf[0][0], in_=u), r=[uk], w=[("Y", 0, 0)])
                            else:
                                Q("act", lambda e, c0=c0, hd=hd: e.activation(out=Sst[0][:, c0, :], in_=Sf[:, hd, :], func=AF.Copy),
                                     r=["Sf"], w=[("Sst", 0, c0)])
                                u, uk = U(c0)
                                Q("dve", lambda e, u=u, hd=hd: e.tensor_tensor(out=Ybuf[0][0], in0=u, in1=Sf[:, hd, :], op=ALU.add),
                                     r=[uk, "Sf"], w=[("Y", 0, 0)])
                            cur = 0
                            for cc in range(cps):
                                c = c0 + cc
                                if cc < cps - 1:
                                    Q("act", lambda e, c=c, cur=cur, dc=Dc(c): e.activation(
                                        out=Sst[0][:, c + 1, :], in_=Ybuf[0][cur], func=AF.Identity, scale=dc),
                                        r=[("Y", 0, cur), pbk], w=[("Sst", 0, c + 1)])
                                    u, uk = U(c + 1)
                                    Q("dve", lambda e, c=c, cur=cur, u=u, dc=Dc(c): e.scalar_tensor_tensor(
                                        out=Ybuf[0][1 - cur], in0=Ybuf[0][cur], scalar=dc, in1=u, op0=ALU.mult, op1=ALU.add),
                                        r=[("Y", 0, cur), pbk, uk], w=[("Y", 0, 1 - cur)])
                                    cur = 1 - cur
                                else:
                                    if kind == "p":
                                        so = Sout[sgm * 2]
                                        Q("dve", lambda e, c=c, cur=cur, so=so, dc=Dc(c): e.tensor_scalar(
                                            out=so, in0=Ybuf[0][cur], scalar1=dc, scalar2=None, op0=ALU.mult),
                                            r=[("Y", 0, cur), pbk], w=[("Sout", sgm * 2)])
                                        Qd(sp_q, ns[j * 2 + sgm, 0, hd], so, r=[("Sout", sgm * 2)], w=[("ns", j, sgm, 0, hd)])
                                    else:
                                        Q("dve", lambda e, c=c, cur=cur, hd=hd, dc=Dc(c): e.tensor_scalar(
                                            out=Sf[:, hd, :], in0=Ybuf[0][cur], scalar1=dc, scalar2=None, op0=ALU.mult),
                                            r=[("Y", 0, cur), pbk, "Sf"], w=["Sf"])
                        else:
                            cur = 0
                            first = True
                            for cc in range(cps - 1, -1, -1):
                                c = c0 + cc
                                u, uk = U(c)
                                if first and kind == "p":
                                    if not prepass:
                                        Q("act", lambda e, c=c: e.activation(out=Sst[1][:, c, :], in_=zerob, func=AF.Copy),
                                             r=["zerob"], w=[("Sst", 1, c)])
                                    Q("dve", lambda e, u=u: e.tensor_copy(out=Ybuf[1][0], in_=u), r=[uk], w=[("Y", 1, 0)])
                                    cur = 0
                                elif first:
                                    if not prepass:
                                        Q("act", lambda e, c=c, hd=hd, dc=Dc(c): e.activation(
                                            out=Sst[1][:, c, :], in_=Sb[:, hd, :], func=AF.Identity, scale=dc),
                                            r=["Sb", pbk], w=[("Sst", 1, c)])
                                    Q("dve", lambda e, c=c, u=u, hd=hd, dc=Dc(c): e.scalar_tensor_tensor(
                                        out=Ybuf[1][0], in0=Sb[:, hd, :], scalar=dc, in1=u, op0=ALU.mult, op1=ALU.add),
                                        r=["Sb", pbk, uk], w=[("Y", 1, 0)])
                                    cur = 0
                                else:
                                    if not prepass:
                                        Q("act", lambda e, c=c, cur=cur, dc=Dc(c): e.activation(
                                            out=Sst[1][:, c, :], in_=Ybuf[1][cur], func=AF.Identity, scale=dc),
                                            r=[("Y", 1, cur), pbk], w=[("Sst", 1, c)])
                                    Q("dve", lambda e, c=c, cur=cur, u=u, dc=Dc(c): e.scalar_tensor_tensor(
                                        out=Ybuf[1][1 - cur], in0=Ybuf[1][cur], scalar=dc, in1=u, op0=ALU.mult, op1=ALU.add),
                                        r=[("Y", 1, cur), pbk, uk], w=[("Y", 1, 1 - cur)])
                                    cur = 1 - cur
                                first = False
                            if kind == "p":
                                so = Sout[sgm * 2 + 1]
                                Q("dve", lambda e, cur=cur, so=so: e.tensor_copy(out=so, in_=Ybuf[1][cur]),
                                     r=[("Y", 1, cur)], w=[("Sout", sgm * 2 + 1)])
                                Qd(sp_q, ns[j * 2 + sgm, 1, hd], so, r=[("Sout", sgm * 2 + 1)], w=[("ns", j, sgm, 1, hd)])
                            elif prepass:
                                Q("dve", lambda e, cur=cur, hd=hd: e.tensor_copy(out=Sb[:, hd, :], in_=Ybuf[1][cur]),
                                     r=[("Y", 1, cur), "Sb"], w=["Sb"])
                for i_ in range(max(len(chains[0]), len(chains[1]))):
                    for d_ in dirs:
                        if i_ < len(chains[d_]):
                            chains[d_][i_]()
                if prepass:
                    return
                for d_ in range(2):
                    sb_ = bank(5 + d_).rearrange("p (b t) -> p b t", b=4)
                    msk = maskf if d_ == 0 else maskb
                    P.op("dve", lambda e, d_=d_, sb_=sb_, msk=msk: e.tensor_tensor(
                        out=Amat[d_], in0=sb_, in1=msk.unsqueeze(1).to_broadcast([128, 4, 128]), op=ALU.mult),
                        r=pk(5 + d_) + ["maskf", "maskb"], w=[("Amat", d_, tb) for tb in range(4)])
                for c in range(8):
                    tb, half = c // 2, c % 2
                    ob = bank(7)[:, c * 64:(c + 1) * 64]
                    hs = slice(half * 64, (half + 1) * 64)
                    for d_ in range(2):
                        P.op("pe", lambda e, d_=d_, tb=tb, hs=hs, ob=ob, hd=hd: e.matmul(
                            ob, lhsT=vtok[:, tb, hd * 128:(hd + 1) * 128], rhs=Amat[d_][:, tb, hs],
                            start=(d_ == 0), stop=False),
                            r=[("vtok", tb, hd // 4), ("Amat", d_, tb)], w=pk(7))
                    for d_ in range(2):
                        P.op("pe", lambda e, d_=d_, c=c, ob=ob, g=g: e.matmul(
                            ob, lhsT=Sst[d_][:, c, :], rhs=qT[g][d_][:, c * 64:(c + 1) * 64], start=False, stop=(d_ == 1)),
                            r=[("Sst", d_, c), ("qT", g, d_)], w=pk(7))
                P.op("act", lambda e: e.activation(out=osq, in_=bank(7), func=AF.Square), r=pk(7), w=["osq"])
                P.op("pe", lambda e: e.matmul(bank(4), lhsT=ones, rhs=osq, start=True, stop=True),
                     r=["ones", "osq"], w=pk(4))
                P.op("act", lambda e: e.activation(out=ort, in_=bank(4), func=AF.Ln, scale=1.0 / 128, bias=epsc),
                     r=pk(4) + ["epsc"], w=["ort"])
                P.op("act", lambda e: e.activation(out=ort, in_=ort, func=AF.Exp, scale=-0.5), r=["ort"], w=["ort"])
                P.op("dve", lambda e: e.tensor_tensor(out=onb, in0=bank(7), in1=ort, op=ALU.mult),
                     r=pk(7) + ["ort"], w=["onb"])
                P.op("dve", lambda e, hd=hd, g=g: e.scalar_tensor_tensor(
                    out=omix[:, hd, :], in0=onb, scalar=hgw, in1=gs[g], op0=ALU.mult, op1=ALU.mult),
                    r=["onb", "hgw", ("gs", g)], w=[("omix", hd)])

            if do_heads:
                S1(0)
                S2(0)
                if prepass:
                    vproj()
                for hd in range(HEADS):
                    if hd + 1 < HEADS:
                        S1(hd + 1)
                    if prepass and hd == 0:
                        pproj_prepass()
                    if do_s3:
                        S3(hd)
                    if hd + 1 < HEADS:
                        S2(hd + 1)
            if prepass:
                if j >= 1:
                    P.dma(io_q, sbscr[j - 1], Sb, r=["Sb"], w=[("sbscr", j - 1)])
            if prepass:
                P.barrier()
            A.release(hm)
            if prepass:
                P.barrier()
                A.release(mk)
                return None
            wpl, wplkey = load_w(wq_pool.rearrange("p g k c -> p (g k c)"), 2048, "wq_pool_all",
                                 "p (g k c) -> p g k c", g=4, k=2)
            for gi in range(4):
                for half in range(2):
                    oc = gi * 2 + half
                    bi = oc % 4
                    for kc in range(2):
                        P.op("pe", lambda e, gi=gi, half=half, kc=kc, bi=bi: e.matmul(
                            bank(bi), lhsT=wpl[:, gi, kc, half * 128:(half + 1) * 128], rhs=dm[:, gi * 2 + kc, :],
                            start=(kc == 0), stop=(kc == 1)), r=[wplkey, ("dm", gi * 2 + kc)], w=pk(bi))
                    P.op("act", lambda e, oc=oc, bi=bi: e.activation(
                        out=omix[:, 8 + oc, :], in_=bank(bi), func=AF.Identity, scale=pscol[:, oc:oc + 1]),
                        r=pk(bi) + ["pscol"], w=[("omix", 8 + oc)])
            return mk, omix

        def tile_main(kind, j, ctx, xsrc, ydst, last_tile):
            res = mixer(kind, j, ctx, xsrc, False)
            mk, omix = res
            P.barrier()
            A.release(mk)
            x1 = A.f32(4 * D).rearrange("p (b c) -> p b c", b=4)
            gbc = A.f32(D)
            h2 = A.bf16(16 * 512).rearrange("p (k t) -> p k t", k=16)
            assert A.mark() == mk + LOW_WORDS
            A.top = mk + LOW_WORDS + 4096
            tmp = [A.f32(512) for _ in range(2)]
            junk = A.bf16(D)
            xn = [A.bf16(D) for _ in range(2)]
            ss = A.f32(4); rt = A.f32(4); rstd = A.f32(4)
            for tb in range(4):
                P.dma(io_q, x1[:, tb, :], xsrc[j * TT + tb * 128: j * TT + (tb + 1) * 128, :], r=[],
                      w=[("x1", tb, cg) for cg in range(4)])
            P.dma(io_q, gbc, modscr[ctx, 2 * D:3 * D].partition_broadcast(128), r=[], w=["gbc"])

            def resid(bset, cg, gv, gkey, tmpl):
                for tb in range(4):
                    t_ = tmpl[tb % 2]
                    P.op("dve", lambda e, tb=tb, t_=t_, cg=cg, bset=bset: e.tensor_tensor(
                        out=t_, in0=bank(bset[tb]), in1=gv[:, cg * 512:(cg + 1) * 512], op=ALU.mult),
                        r=pk(bset[tb]) + [gkey], w=[("tmp", tb % 2)])
                    P.op("dve", lambda e, tb=tb, t_=t_, cg=cg: e.tensor_tensor(
                        out=x1[:, tb, cg * 512:(cg + 1) * 512], in0=x1[:, tb, cg * 512:(cg + 1) * 512], in1=t_, op=ALU.add),
                        r=[("tmp", tb % 2), ("x1", tb, cg)], w=[("x1", tb, cg)])

            for cg in range(4):
                wv, wkey = load_w(wq_out[cg].rearrange("p k c -> p (k c)"), 8192, ("wq_out", cg), "p (k c) -> p k c", k=16)
                bset = [(cg % 2) * 4 + t for t in range(4)]
                proj_tm(bset, wv, wkey, 16, 0, omix, lambda kc, tb: [("omix", kc)], True, True)
                resid(bset, cg, gbc, "gbc", tmp)

            def get_src(tb):
                return x1[:, tb, :], [("x1", tb, cg) for cg in range(4)]
            norm_phase(get_src, ctx, 1, h2, "h2", xn, ss, rt, rstd, junk)
            if kind == "s" and not last_tile:
                P.dma(io_q, x1def[j:j + 1, :], x1[127:128, 3, :], r=[("x1", 3, cg) for cg in range(4)], w=[("x1def", j)])
            b2_alias = [("omix", kc_) for kc_ in range(16)] + [("tmp", 0), ("tmp", 1), ("xn", 0), ("xn", 1)]
            A.top = mk + LOW_WORDS
            act = A.bf16(NFC * 512).rearrange("p (k t) -> p k t", k=NFC)
            acc = [[A.f32(512) for _ in range(2)] for _ in range(2)]
            sa = [A.f32(512) for _ in range(2)]
            tmp2 = [A.f32(512) for _ in range(2)]
            fbc = A.f32(D)
            junk2 = A.bf16(D)
            ss2 = A.f32(4); rt2 = A.f32(4); rs2 = A.f32(4)
            nseq = 2 if kind == "p" else 1
            L_ = 512 // nseq
            for i in range(NFC):
                s_, hh = i // 2, i % 2
                if hh == 0:
                    wv, wkey = load_w(wq_up[s_].rearrange("p t k c -> p (t k c)"), 8192, ("wq_up_all", s_),
                                      "p (t k c) -> p t k c", t=2, k=16)
                    wkeep = (wv, wkey)
                wv, wkey = wkeep
                pair = (i % 4) * 2
                for ab in range(2):
                    bi = pair + ab
                    fc = i + ab * NFC
                    proj_fm(bi, wv[:, ab, :, hh * 128:(hh + 1) * 128], wkey, h2, "h2")
                    ac = acc[i % 2][ab]
                    akey = ("acc", i % 2, ab)
                    P.op("act", lambda e, bi=bi, ac=ac, fc=fc: e.activation(
                        out=ac, in_=bank(bi), func=AF.Identity, scale=cwcol[:, 1, fc:fc + 1], bias=cbcol[:, fc:fc + 1]),
                        r=pk(bi) + [("cwcol", 1), "cbcol"], w=[akey])
                    a3 = ac.rearrange("p (s t) -> p s t", s=nseq)
                    b3 = bank(bi).rearrange("p (s t) -> p s t", s=nseq)
                    P.op("dve", lambda e, a3=a3, b3=b3, fc=fc: e.scalar_tensor_tensor(
                        out=a3[:, :, 1:L_], in0=b3[:, :, 0:L_ - 1], scalar=cwcol[:, 0, fc:fc + 1], in1=a3[:, :, 1:L_],
                        op0=ALU.mult, op1=ALU.add), r=pk(bi) + [akey, ("cwcol", 0)], w=[akey])
                    P.op("dve", lambda e, a3=a3, b3=b3, fc=fc: e.scalar_tensor_tensor(
                        out=a3[:, :, 0:L_ - 1], in0=b3[:, :, 1:L_], scalar=cwcol[:, 2, fc:fc + 1], in1=a3[:, :, 0:L_ - 1],
                        op0=ALU.mult, op1=ALU.add), r=pk(bi) + [akey, ("cwcol", 2)], w=[akey])
                    if kind == "s":
                        if j >= 1:
                            P.op("dve", lambda e, ac=ac, fc=fc: e.scalar_tensor_tensor(
                                out=ac[:, 0:1], in0=udef[:, fc, j - 1, 1:2], scalar=cwcol[:, 0, fc:fc + 1], in1=ac[:, 0:1],
                                op0=ALU.mult, op1=ALU.add), r=[akey, ("udef", fc, j - 1, 1), ("cwcol", 0)], w=[akey])
                            P.op("act", lambda e, bi=bi, fc=fc: e.activation(
                                out=udef[:, fc, j - 1, 2:3], in_=bank(bi)[:, 0:1], func=AF.Copy),
                                r=pk(bi, 0, 1), w=[("udef", fc, j - 1, 2)])
                        if not last_tile:
                            P.op("act", lambda e, bi=bi, fc=fc: e.activation(
                                out=udef[:, fc, j, 0:2], in_=bank(bi)[:, 510:512], func=AF.Copy),
                                r=pk(bi, 3, 4), w=[("udef", fc, j, 1)])
                P.op("act", lambda e, i=i: e.activation(out=sa[i % 2], in_=acc[i % 2][0], func=AF.Silu),
                     r=[("acc", i % 2, 0)], w=[("sa", i % 2)])
                P.op("dve", lambda e, i=i: e.tensor_tensor(out=act[:, i, :], in0=sa[i % 2], in1=acc[i % 2][1], op=ALU.mult),
                     r=[("sa", i % 2), ("acc", i % 2, 1)], w=[("act", i)] + (b2_alias if i == 0 else []))
            P.dma(io_q, gbc, modscr[ctx, 5 * D:6 * D].partition_broadcast(128), r=["gbc"], w=["gbc"])
            P.dma(io_q, fbc, final_norm_w.partition_broadcast(128), r=[], w=["fbc"])
            for cg in range(4):
                bset = [(cg % 2) * 4 + t for t in range(4)]
                for part in range(4):
                    wv, wkey = load_w(wq_down[cg, part].rearrange("p k c -> p (k c)"), 11 * 512, ("wq_down", cg, part),
                                      "p (k c) -> p k c", k=11)
                    proj_tm(bset, wv, wkey, 11, part * 11, act, lambda kc, tb: [("act", kc)], part == 0, part == 3)
                resid(bset, cg, gbc, "gbc", tmp2)
            for tb in range(4):
                xk = [("x1", tb, cg) for cg in range(4)]
                P.op("act", lambda e, tb=tb: e.activation(out=junk2, in_=x1[:, tb, :], func=AF.Square,
                                                          accum_out=ss2[:, tb:tb + 1]), r=xk, w=[("ss2", tb)])
                P.op("act", lambda e, tb=tb: e.activation(out=rt2[:, tb:tb + 1], in_=ss2[:, tb:tb + 1], func=AF.Sqrt,
                                                          scale=1.0 / D, bias=epsc), r=[("ss2", tb), "epsc"], w=[("rt2", tb)])
                P.op("dve", lambda e, tb=tb: e.reciprocal(out=rs2[:, tb:tb + 1], in_=rt2[:, tb:tb + 1]),
                     r=[("rt2", tb)], w=[("rs2", tb)])
                P.op("dve", lambda e, tb=tb: e.scalar_tensor_tensor(
                    out=x1[:, tb, :], in0=x1[:, tb, :], scalar=rs2[:, tb:tb + 1], in1=fbc, op0=ALU.mult, op1=ALU.mult),
                    r=xk + [("rs2", tb), "fbc"], w=xk)
                nrow = 128
                if kind == "s" and not last_tile and tb == 3:
                    nrow = 127
                P.dma(sp_q, ydst[j * TT + tb * 128: j * TT + tb * 128 + nrow, :], x1[0:nrow, tb, :], r=xk, w=[("y", kind, j, tb)],
                      bg=True)
            P.barrier()
            A.release(mk)

        def deferred():
            nb = nst - 1
            if nb <= 0:
                return
            P.barrier(full=True)
            mk = A.mark()
            actd = A.bf16(NFC * 8).rearrange("p (k t) -> p k t", k=NFC)
            cv = A.f32(88 * 8).rearrange("p (c j) -> p c j", c=88)
            t1 = A.f32(88 * 8).rearrange("p (c j) -> p c j", c=88)
            sad = A.f32(NFC * 8).rearrange("p (c j) -> p c j", c=NFC)
            xd = A.f32(D)
            g2 = A.f32(D)
            fb = A.f32(D)
            junk = A.bf16(D)
            tmpd = A.f32(512)
            ssd = A.f32(1); rtd = A.f32(1); rsd = A.f32(1)
            ctx = 1
            alld = []
            for jj in range(nb):
                P.op("dve", lambda e, jj=jj: e.tensor_tensor(out=cv[:, :, jj], in0=udef[:, :, jj, 0], in1=cwcol[:, 0, :], op=ALU.mult),
                     r=alld + [("cwcol", 0)], w=[("cv", jj)])
                for e_ in (1, 2):
                    P.op("dve", lambda e, jj=jj, e_=e_: e.tensor_tensor(out=t1[:, :, jj], in0=udef[:, :, jj, e_], in1=cwcol[:, e_, :], op=ALU.mult),
                         r=alld + [("cwcol", e_)], w=[("t1", jj)])
                    P.op("dve", lambda e, jj=jj: e.tensor_tensor(out=cv[:, :, jj], in0=cv[:, :, jj], in1=t1[:, :, jj], op=ALU.add),
                         r=[("cv", jj), ("t1", jj)], w=[("cv", jj)])
                P.op("dve", lambda e, jj=jj: e.tensor_tensor(out=cv[:, :, jj], in0=cv[:, :, jj], in1=cbcol, op=ALU.add),
                     r=[("cv", jj), "cbcol"], w=[("cv", jj)])
                P.op("act", lambda e, jj=jj: e.activation(out=sad[:, :, jj], in_=cv[:, 0:NFC, jj], func=AF.Silu),
                     r=[("cv", jj)], w=[("sad", jj)])
                P.op("dve", lambda e, jj=jj: e.tensor_tensor(out=actd[:, :, jj], in0=sad[:, :, jj], in1=cv[:, NFC:88, jj], op=ALU.mult),
                     r=[("sad", jj), ("cv", jj)], w=["actd"])
            P.dma(io_q, xd[0:nb, :], x1def[0:nb, :], r=[("x1def", jj) for jj in range(nb)], w=["xd"])
            P.dma(io_q, g2, modscr[ctx, 5 * D:6 * D].partition_broadcast(128), r=[], w=["g2d"])
            P.dma(io_q, fb, final_norm_w.partition_broadcast(128), r=[], w=["fbd"])
            for cg in range(4):
                bi = cg
                for part in range(4):
                    wv, wkey = load_w(wq_down[cg, part].rearrange("p k c -> p (k c)"), 11 * 512, ("wq_down", cg, part),
                                      "p (k c) -> p k c", k=11)
                    for kl in range(11):
                        kc = part * 11 + kl
                        P.op("pe", lambda e, kl=kl, kc=kc, wv=wv, bi=bi, part=part: e.matmul(
                            bank(bi)[0:nb, :], lhsT=actd[:, kc, 0:nb], rhs=wv[:, kl, :],
                            start=(part == 0 and kl == 0), stop=(part == 3 and kl == 10)),
                            r=[wkey, "actd"], w=pk(bi))
                P.op("dve", lambda e, bi=bi, cg=cg: e.tensor_tensor(
                    out=tmpd[0:nb, :], in0=bank(bi)[0:nb, :], in1=g2[0:nb, cg * 512:(cg + 1) * 512], op=ALU.mult),
                    r=pk(bi) + ["g2d"], w=["tmpd"])
                P.op("dve", lambda e, cg=cg: e.tensor_tensor(
                    out=xd[0:nb, cg * 512:(cg + 1) * 512], in0=xd[0:nb, cg * 512:(cg + 1) * 512], in1=tmpd[0:nb, :], op=ALU.add),
                    r=["tmpd", "xd"], w=["xd"])
            P.op("act", lambda e: e.activation(out=junk[0:nb, :], in_=xd[0:nb, :], func=AF.Square, accum_out=ssd[0:nb, :]),
                 r=["xd"], w=["ssd"])
            P.op("act", lambda e: e.activation(out=rtd[0:nb, :], in_=ssd[0:nb, :], func=AF.Sqrt, scale=1.0 / D, bias=epsc[0:nb, :]),
                 r=["ssd", "epsc"], w=["rtd"])
            P.op("dve", lambda e: e.reciprocal(out=rsd[0:nb, :], in_=rtd[0:nb, :]), r=["rtd"], w=["rsd"])
            P.op("dve", lambda e: e.scalar_tensor_tensor(
                out=xd[0:nb, :], in0=xd[0:nb, :], scalar=rsd[0:nb, :], in1=fb[0:nb, :], op0=ALU.mult, op1=ALU.mult),
                r=["xd", "rsd", "fbd"], w=["xd"])
            for jj in range(nb):
                P.dma(io_q, ys[jj * TT + 511: jj * TT + 512, :], xd[jj:jj + 1, :], r=["xd"], w=[("ydef", jj)])
            P.barrier()
            A.release(mk)

        if stages >= 1:
            for j in range(nst - 1, -1, -1):
                mixer("s", j, 1, xs, True)
        issue_conv(len(conv_list))
        if stages >= 2:
            for j in range(npt):
                tile_main("p", j, 0, xp, yp, True)
        if stages >= 3:
            for j in range(nst):
                tile_main("s", j, 1, xs, ys, j == nst - 1)
            deferred()
        counts = P.emit(csem, dsem)
    return nc, counts


_CACHE = {}


def _get_program(npt, nst):
    key = (npt, nst)
    if key not in _CACHE:
        _CACHE[key] = build_program(npt, nst)
    return _CACHE[key]


def make_in_map(core, npt, nst, x_prompt, x_sample, state_hgrn, c, c_ctx, w_ada, b_ada, norm1_w, w_in,
                lb_param, hg_norm_w, w_pool, pool_scale, w_out, norm2_w, w_up, conv_w, conv_b, w_down,
                final_norm_w, consts):
    f = lambda a: np.ascontiguousarray(np.asarray(a, dtype=np.float32))
    nseq = npt * 2
    m = {}
    if npt > 0:
        m["xp"] = f(x_prompt[core * nseq:(core + 1) * nseq]).reshape(nseq * 256, D)
    else:
        m["xp"] = np.zeros((TT, D), np.float32)
    m["xs"] = f(x_sample[core]).reshape(-1, D)
    m["st"] = f(state_hgrn[core, 0])
    m["cvec"] = np.ascontiguousarray(np.stack([np.asarray(c_ctx, np.float32), np.asarray(c[core], np.float32)]))
    m["w_ada"] = f(w_ada[0]); m["b_ada"] = f(b_ada[0]); m["norm1_w"] = f(norm1_w[0]); m["w_in"] = f(w_in[0])
    m["lb_param"] = f(lb_param); m["hg_norm_w"] = f(hg_norm_w[0]); m["w_pool"] = f(w_pool[0])
    m["pool_scale"] = f(pool_scale[0]); m["w_out"] = f(w_out[0]); m["norm2_w"] = f(norm2_w[0])
    m["w_up"] = f(w_up[0]); m["conv_w"] = f(conv_w[0]); m["conv_b"] = f(conv_b[0]); m["w_down"] = f(w_down[0])
    m["final_norm_w"] = f(final_norm_w)
    m.update(consts)
    return m


def kernel(x_prompt, x_sample, state_hgrn, c, c_ctx, w_ada, b_ada, norm1_w, w_in, lb_param, hg_norm_w,
           w_pool, pool_scale, w_out, norm2_w, w_up, conv_w, conv_b, w_down, final_norm_w):
    ncores = 8
    x_prompt = np.asarray(x_prompt); x_sample = np.asarray(x_sample)
    B, S, _ = x_prompt.shape
    DB, DS, _ = x_sample.shape
    assert DB == ncores and B % ncores == 0 and S == 256
    npt = (B // ncores) // 2
    nst = DS // TT
    nc, _ = _get_program(npt, nst)
    consts = host_constants(nst)
    in_maps = [make_in_map(k, npt, nst, x_prompt, x_sample, np.asarray(state_hgrn), np.asarray(c), np.asarray(c_ctx),
                           np.asarray(w_ada), np.asarray(b_ada), np.asarray(norm1_w), np.asarray(w_in),
                           np.asarray(lb_param), np.asarray(hg_norm_w), np.asarray(w_pool), np.asarray(pool_scale),
                           np.asarray(w_out), np.asarray(norm2_w), np.asarray(w_up), np.asarray(conv_w),
                           np.asarray(conv_b), np.asarray(w_down), np.asarray(final_norm_w), consts)
               for k in range(ncores)]
    res = run_bass_kernel_spmd(nc, in_maps, core_ids=list(range(ncores)))
    outs = res.results
    nseq = npt * 2
    y_prompt = np.concatenate([np.asarray(o["yp"], np.float32).reshape(nseq, 256, D) for o in outs], axis=0)
    y_sample = np.stack([np.asarray(o["ys"], np.float32).reshape(DS, D) for o in outs], axis=0)
    new_state = np.concatenate([np.asarray(o["ns"], np.float32).reshape(nseq, 1, 2, HEADS, 128, 128) for o in outs], axis=0)
    return (y_prompt, y_sample, new_state)
```

```python
import numpy as np
import ml_dtypes
import concourse.bass as bass
import concourse.mybir as mybir
from concourse.bass_utils import run_bass_kernel_spmd

F32 = mybir.dt.float32
BF16 = mybir.dt.bfloat16
AF = mybir.ActivationFunctionType
ALU = mybir.AluOpType

EPS = 1e-6
D = 2048
NKC = 16
HEADS = 8
DFF = 5632
NFC = 44
TT = 512
CH = 64
NDMASEM = 8


class Prog:
    ENGS = ("pe", "act", "dve", "pool", "sp")

    def __init__(self, nc):
        self.nc = nc
        self.eng = {"pe": nc.tensor, "act": nc.scalar, "dve": nc.vector, "pool": nc.gpsimd, "sp": nc.sync}
        self.ops = []
        self.ncomp = {e: 0 for e in self.ENGS}
        self.comp_ops = {e: [] for e in self.ENGS}
        self.writers = {}
        self.readers = {}
        self.known = {e: {} for e in self.ENGS}
        self.known_dma = {e: set() for e in self.ENGS}
        self.ndma = {"sp": 0, "pool": 0, "act": 0}
        self.bg = set()

    def _deps(self, eng, is_dma, r, w):
        raw = []
        other = []
        for k in r:
            raw += self.writers.get(k, [])
        for k in w:
            raw += self.writers.get(k, [])
            other += self.readers.get(k, [])
        waits = []
        best = {}
        dmas = []
        for kind, toks in (("raw", raw), ("war", other)):
            for t in toks:
                if t[0] == "c":
                    _, e2, idx = t
                    if e2 == eng and not is_dma:
                        if eng == "pe":
                            continue
                    if idx > best.get(e2, -1):
                        best[e2] = idx
                else:
                    dmas.append(t)
        for e2, idx in best.items():
            if self.known[eng].get(e2, -1) >= idx:
                continue
            self.known[eng][e2] = idx
            waits.append(("c", e2, idx))
        for t in dmas:
            if t in self.known_dma[eng]:
                continue
            self.known_dma[eng].add(t)
            waits.append(t)
        return waits

    def _commit(self, tok, r, w):
        for k in w:
            self.writers[k] = [tok]
            self.readers[k] = []
        for k in r:
            self.readers.setdefault(k, []).append(tok)

    LIMIT = None

    def op(self, eng, fn, r=(), w=()):
        if Prog.LIMIT is not None and len(self.ops) >= Prog.LIMIT:
            return None
        waits = self._deps(eng, False, r, w)
        idx = self.ncomp[eng]
        self.ncomp[eng] += 1
        tok = ("c", eng, idx)
        rec = dict(eng=eng, fn=fn, waits=waits, dma=None, inc=False, idx=idx)
        self.ops.append(rec)
        self.comp_ops[eng].append(rec)
        self._commit(tok, r, w)
        return tok

    def dma(self, q, out, in_, r=(), w=(), bg=False, **kw):
        if Prog.LIMIT is not None and len(self.ops) >= Prog.LIMIT:
            return None
        waits = self._deps(q, True, r, w)
        n = self.ndma[q]
        self.ndma[q] += 1
        tok = ("d", q, n)
        if bg:
            self.bg.add(tok)
        if n >= NDMASEM:
            prev = ("d", q, n - NDMASEM)
            if prev not in self.known_dma[q]:
                self.known_dma[q].add(prev)
                waits.append(prev)

        def fn(e, out=out, in_=in_, kw=kw):
            return e.dma_start(out=out, in_=in_, **kw)

        rec = dict(eng=q, fn=fn, waits=waits, dma=tok, inc=False, idx=None)
        self.ops.append(rec)
        self._commit(tok, r, w)
        return tok

    def barrier(self, full=False):
        if Prog.LIMIT is not None and len(self.ops) >= Prog.LIMIT:
            return
        toks = []
        for e in self.ENGS:
            if self.ncomp[e] > 0:
                toks.append(("c", e, self.ncomp[e] - 1))
        for q, n in self.ndma.items():
            for i in range(max(0, n - NDMASEM), n):
                if full or ("d", q, i) not in self.bg:
                    toks.append(("d", q, i))
        for e in self.ENGS:
            if e in ("sp", "pe") and not full:
                continue
            waits = []
            for t in toks:
                if t[0] == "c":
                    if t[1] == e and e == "pe":
                        continue
                    if self.known[e].get(t[1], -1) >= t[2]:
                        continue
                    self.known[e][t[1]] = t[2]
                    waits.append(t)
                else:
                    if t in self.known_dma[e]:
                        continue
                    self.known_dma[e].add(t)
                    waits.append(t)
            if waits:
                self.ops.append(dict(eng=e, fn=None, waits=waits, dma=None, inc=False, idx=None))
        def keep(k):
            return not full
        self.writers = {k: v for k, v in self.writers.items() if keep(k)}
        self.readers = {k: v for k, v in self.readers.items() if keep(k)}

    def emit(self, csem, dsem):
        for rec in self.ops:
            for t in rec["waits"]:
                if t[0] == "c":
                    self.comp_ops[t[1]][t[2]]["inc"] = True
        count = {}
        for e in self.ENGS:
            c = 0
            for rec in self.comp_ops[e]:
                if rec["inc"]:
                    c += 1
                rec["cnt"] = c
            count[e] = c
        for rec in self.ops:
            e = self.eng[rec["eng"]]
            for t in rec["waits"]:
                if t[0] == "c":
                    e.wait_ge(csem[t[1]], self.comp_ops[t[1]][t[2]]["cnt"])
                else:
                    _, q, n = t
                    e.wait_ge(dsem[q][n % NDMASEM], 16 * (n // NDMASEM + 1))
            if rec["fn"] is None:
                continue
            ins = rec["fn"](e)
            if rec["dma"] is not None:
                _, q, n = rec["dma"]
                ins.then_inc(dsem[q][n % NDMASEM], 16)
            elif rec["inc"]:
                ins.then_inc(csem[rec["eng"]], 1)
        sp = self.eng["sp"]
        for q, n in self.ndma.items():
            for i in range(max(0, n - NDMASEM), n):
                sp.wait_ge(dsem[q][i % NDMASEM], 16 * (i // NDMASEM + 1))
        return count


class Arena:
    DEBUG = False

    def __init__(self, ap, nwords):
        self.ap = ap
        self.n = nwords
        self.top = 0

    LOG = []

    def f32(self, n):
        off = self.top
        self.top += n
        assert self.top <= self.n, ("arena overflow", self.top, self.n)
        if Arena.DEBUG:
            import inspect
            fr = inspect.stack()
            ctxs = [f.code_context[0].strip() for f in fr[1:3] if f.code_context]
            Arena.LOG.append((off, n, " | ".join(ctxs)))
        return self.ap[:, off:off + n]

    def bf16(self, n):
        w = (n + 1) // 2
        return self.f32(w).bitcast(BF16)[:, 0:n]

    def mark(self):
        return self.top

    def release(self, m):
        self.top = m


def _inv_counts_1d(L, w):
    t = np.arange(L)
    lo = np.clip(t - w // 2, 0, L)
    hi = np.clip(t + (w - w // 2), 0, L)
    return 1.0 / (hi - lo).astype(np.float64)


def host_constants(nst):
    bf = ml_dtypes.bfloat16
    c = {}
    c["k_ident"] = np.eye(128, dtype=np.float32).astype(bf)
    c["k_ones"] = np.ones((128, 128), np.float32).astype(bf)
    s = np.arange(128)[:, None]
    t = np.arange(128)[None, :]
    same = (s // CH) == (t // CH)
    c["k_maskf"] = (same & (s <= t)).astype(np.float32).astype(bf)
    c["k_maskb"] = (same & (s >= t)).astype(np.float32).astype(bf)
    ind = np.full(8 * 65, 1e-30, np.float32)
    ind[::65] = 1.0
    c["k_ind"] = ind
    ip = np.zeros((4, 512), np.float32)
    for wi, w in enumerate((2, 4, 8, 16)):
        v = _inv_counts_1d(256, w)
        ip[wi] = np.concatenate([v, v]).astype(np.float32)
    c["k_invp"] = ip
    rows = max(nst, 1) * 8
    isx = np.zeros((4, rows * 64), np.float32)
    for wi, w in enumerate((2, 4, 8, 16)):
        ir = _inv_counts_1d(rows, w)
        ic = _inv_counts_1d(64, w)
        isx[wi] = (ir[:, None] * ic[None, :]).reshape(-1).astype(np.float32)
    c["k_invs"] = isx
    return c


def build_program(npt, nst, debug=False, stages=3):
    nc = bass.Bass("TRN2", target_bir_lowering=False)
    P = Prog(nc)
    NTOKS = nst * TT

    def din(name, shape, dt=F32):
        return nc.dram_tensor(name, list(shape), dt, kind="ExternalInput").ap()

    def dout(name, shape, dt=F32):
        return nc.dram_tensor(name, list(shape), dt, kind="ExternalOutput").ap()

    def dscr(name, shape, dt=F32):
        return nc.dram_tensor(name, list(shape), dt, kind="Internal").ap()

    xp = din("xp", [max(npt, 1) * TT, D])
    xs = din("xs", [max(NTOKS, TT), D])
    st = din("st", [2, HEADS, 128, 128])
    cvec = din("cvec", [2, D])
    w_ada = din("w_ada", [D, 6 * D])
    b_ada = din("b_ada", [6 * D])
    norm1_w = din("norm1_w", [D])
    w_in = din("w_in", [D, 6144])
    lb_param = din("lb_param", [2, 2, 1024])
    hg_norm_w = din("hg_norm_w", [128])
    w_pool = din("w_pool", [4, 256, 256])
    pool_scale = din("pool_scale", [1024])
    w_out = din("w_out", [D, D])
    norm2_w = din("norm2_w", [D])
    w_up = din("w_up", [D, 2 * DFF])
    conv_w = din("conv_w", [3, 2 * DFF])
    conv_b = din("conv_b", [2 * DFF])
    w_down = din("w_down", [DFF, D])
    final_norm_w = din("final_norm_w", [D])
    k_ident = din("k_ident", [128, 128], BF16)
    k_ones = din("k_ones", [128, 128], BF16)
    k_maskf = din("k_maskf", [128, 128], BF16)
    k_maskb = din("k_maskb", [128, 128], BF16)
    k_ind = din("k_ind", [520])
    k_invp = din("k_invp", [4, 512])
    k_invs = din("k_invs", [4, max(nst, 1) * 512])
    yp = dout("yp", [max(npt, 1) * TT, D])
    ys = dout("ys", [max(NTOKS, TT), D])
    ns = dout("ns", [max(npt, 1) * 2, 2, HEADS, 128, 128])
    wq_head = dscr("wq_head", [4, 2, 128, 2, 16, 256], BF16)
    wq_v = dscr("wq_v", [2, 128, 16, 512], BF16)
    wq_p = dscr("wq_p", [2, 128, 2, 16, 256], BF16)
    wq_pool = dscr("wq_pool", [128, 4, 2, 256], BF16)
    wq_out = dscr("wq_out", [4, 128, 16, 512], BF16)
    wq_up = dscr("wq_up", [22, 128, 2, 16, 256], BF16)
    wq_down = dscr("wq_down", [4, 4, 128, 11, 512], BF16)
    modscr = dscr("modscr", [2, 6 * D])
    pscr = dscr("pscr", [8, 128, max(NTOKS, TT)], BF16)
    sbscr = dscr("sbscr", [max(nst, 1), 128, HEADS, 128])
    x1def = dscr("x1def", [8, D])
    hscr = dscr("hscr", [max(nst, 1), 128, 16 * 512], BF16)
    bscr = dscr("bscr", [max(nst, 1), HEADS, 128, 1296])
    vscr = dscr("vscr", [max(nst, 1), 128, 4 * 1024], BF16)

    ARENA_WORDS = 53200
    import contextlib
    with contextlib.ExitStack() as es:
        arena_t = es.enter_context(nc.sbuf_tensor("arena", [128, ARENA_WORDS], F32))
        banks = [es.enter_context(nc.psum_tensor(f"psb{i}", [128, 512], F32)) for i in range(8)]
        csem = {e: es.enter_context(nc.semaphore(f"c_{e}")) for e in Prog.ENGS}
        dsem = {q: [es.enter_context(nc.semaphore(f"d_{q}{i}")) for i in range(NDMASEM)]
                for q in ("sp", "pool", "act")}
        A = Arena(arena_t[:], ARENA_WORDS)

        def bank(i):
            return banks[i][:]

        def bankb(i):
            return banks[i][:].bitcast(BF16)

        def pk(i, lo=0, hi=4):
            return [("ps", i)]

        ident = A.bf16(128)
        ones = A.bf16(128)
        maskf = A.bf16(128)
        maskb = A.bf16(128)
        ind = A.f32(520)
        epsc = A.f32(1)
        zerob = A.bf16(128)
        n1col = A.f32(16)
        n2col = A.f32(16)
        modc = A.f32(2 * 4 * 16).rearrange("p (c v k) -> p c v k", c=2, v=4)
        acol = A.f32(2 * 2 * 16).rearrange("p (c v k) -> p c v k", c=2, v=2)
        lbc = A.f32(16).rearrange("p (d h) -> p d h", d=2)
        omlc = A.f32(16).rearrange("p (d h) -> p d h", d=2)
        nomlc = A.f32(16).rearrange("p (d h) -> p d h", d=2)
        hgw = A.f32(1)
        pscol = A.f32(8)
        cwcol = A.f32(3 * 88).rearrange("p (j c) -> p j c", j=3)
        cbcol = A.f32(88)
        Sf = A.f32(HEADS * 128).rearrange("p (h v) -> p h v", h=HEADS)
        Sb = A.f32(HEADS * 128).rearrange("p (h v) -> p h v", h=HEADS)
        udef = A.f32(88 * 8 * 3).rearrange("p (c j e) -> p c j e", c=88, j=8)
        wslots = [A.bf16(8192) for _ in range(3)]
        wctr = [0]
        PERSIST_TOP = A.mark()
        LOW_WORDS = 14336

        sp_q = "sp"
        io_q = "pool"

        def load_w(src_ap, nelem, src_key, shape_str=None, **shape_kw):
            i = wctr[0] % 3
            wctr[0] += 1
            dst = wslots[i][:, 0:nelem]
            P.dma(sp_q, dst, src_ap, r=[src_key], w=[("wslot", i)])
            v = dst
            if shape_str is not None:
                v = dst.rearrange(shape_str, **shape_kw)
            return v, ("wslot", i)

        def ld(dst, src, key, q=io_q, **kw):
            P.dma(q, dst, src, r=[], w=[key], **kw)

        ld(ident, k_ident, "ident")
        ld(ones, k_ones, "ones")
        ld(maskf, k_maskf, "maskf")
        ld(maskb, k_maskb, "maskb")
        ld(ind, k_ind.partition_broadcast(128), "ind")
        P.op("dve", lambda e: e.memset(epsc, EPS), w=["epsc"])
        P.op("dve", lambda e: e.memset(zerob, 0.0), w=["zerob"])
        P.op("dve", lambda e: e.memset(udef.rearrange("p c j e -> p (c j e)"), 0.0), w=["udef"])
        with nc.allow_non_contiguous_dma(reason="tiny per-feature parameter columns"):
            ld(n1col, norm1_w.rearrange("(k p) -> p k", p=128), "n1col", allow_slow_non_contiguous=True)
            ld(n2col, norm2_w.rearrange("(k p) -> p k", p=128), "n2col", allow_slow_non_contiguous=True)
            ld(hgw, hg_norm_w.rearrange("(p o) -> p o", o=1), "hgw", allow_slow_non_contiguous=True)
            ld(pscol, pool_scale.rearrange("(k p) -> p k", p=128), "pscol", allow_slow_non_contiguous=True)
            for j in range(3):
                ld(cwcol[:, j, :], conv_w[j].rearrange("(k p) -> p k", p=128), ("cwcol", j), allow_slow_non_contiguous=True)
            ld(cbcol, conv_b.rearrange("(k p) -> p k", p=128), "cbcol", allow_slow_non_contiguous=True)
            m0 = A.mark()
            lp = A.f32(32).rearrange("p (d l h) -> p d l h", d=2, l=2)
            for d_ in range(2):
                for l_ in range(2):
                    ld(lp[:, d_, l_, :], lb_param[d_, l_].rearrange("(h p) -> p h", p=128), ("lp", d_, l_), allow_slow_non_contiguous=True)
        lpk = [("lp", a, b) for a in range(2) for b in range(2)]
        P.op("dve", lambda e: e.tensor_tensor(out=lbc, in0=lp[:, :, 0, :], in1=lp[:, :, 1, :], op=ALU.subtract),
             r=lpk, w=["lbc"])
        P.op("act", lambda e: e.activation(out=lbc, in_=lbc, func=AF.Sigmoid), r=["lbc"], w=["lbc"])
        P.op("dve", lambda e: e.tensor_scalar(out=omlc, in0=lbc, scalar1=-1.0, scalar2=1.0, op0=ALU.mult, op1=ALU.add),
             r=["lbc"], w=["omlc"])
        P.op("dve", lambda e: e.tensor_scalar(out=nomlc, in0=omlc, scalar1=-1.0, scalar2=None, op0=ALU.mult),
             r=["omlc"], w=["nomlc"])

        conv_list = []

        def cvt(dst, src, key):
            conv_list.append((dst, src, key))

        def issue_conv(n):
            for _ in range(min(n, len(conv_list))):
                dst, src, key = conv_list.pop(0)
                P.dma("pool", dst, src, r=[], w=[key], bg=True)

        def statsrc(w_ap, r0, nrows, c0, ncols):
            return w_ap[r0:r0 + nrows, c0:c0 + ncols].rearrange("(kc p) c -> p kc c", p=128)

        for cg in range(2):
            cvt(wq_v[cg], statsrc(w_in, 0, D, 3072 + cg * 512, 512), ("wq_v", cg))
        hbase = [0, 1024, 2048, 4096]
        for hp in range(4):
            for t_ in range(2):
                cvt(wq_head[hp, 1, :, t_], statsrc(w_in, 0, D, hbase[2 + t_] + hp * 256, 256), ("wq_head", hp, 1))
        for s_ in range(2):
            for t_ in range(2):
                cvt(wq_p[s_, :, t_], statsrc(w_in, 0, D, 5120 + (s_ * 2 + t_) * 256, 256), ("wq_p", s_))
        N_CONV_FIRST = len(conv_list)
        for hp in range(4):
            for t_ in range(2):
                cvt(wq_head[hp, 0, :, t_], statsrc(w_in, 0, D, hbase[t_] + hp * 256, 256), ("wq_head", hp, 0))
        for gi in range(4):
            cvt(wq_pool[:, gi], w_pool[gi].rearrange("(kc p) c -> p kc c", p=128), "wq_pool_all")
        for cg in range(4):
            cvt(wq_out[cg], statsrc(w_out, 0, D, cg * 512, 512), ("wq_out", cg))
        for s_ in range(22):
            cvt(wq_up[s_, :, 0], statsrc(w_up, 0, D, s_ * 256, 256), ("wq_up_all", s_))
            cvt(wq_up[s_, :, 1], statsrc(w_up, 0, D, DFF + s_ * 256, 256), ("wq_up_all", s_))
        for cg in range(4):
            for part in range(4):
                cvt(wq_down[cg, part], statsrc(w_down, part * 1408, 1408, cg * 512, 512), ("wq_down", cg, part))

        m1 = A.mark()
        ccol = A.f32(32).rearrange("p (c k) -> p c k", c=2)
        csil = A.f32(32).rearrange("p (c k) -> p c k", c=2)
        rep = A.bf16(2 * 16 * 128).rearrange("p (c k m) -> p c k m", c=2, k=16)
        brow = [A.f32(512) for _ in range(2)]
        mrow = [A.f32(512) for _ in range(4)]
        with nc.allow_non_contiguous_dma(reason="conditioning vector as columns"):
            for c_ in range(2):
                ld(ccol[:, c_, :], cvec[c_].rearrange("(k p) -> p k", p=128), ("ccol", c_), allow_slow_non_contiguous=True)
        P.op("act", lambda e: e.activation(out=csil, in_=ccol, func=AF.Silu), r=[("ccol", 0), ("ccol", 1)], w=["csil"])
        for c_ in range(2):
            for k_ in range(16):
                P.op("dve", lambda e, c_=c_, k_=k_: e.tensor_scalar(
                    out=rep[:, c_, k_, :], in0=ones, scalar1=csil[:, c_, k_:k_ + 1], scalar2=None, op0=ALU.mult),
                    r=["csil", "ones"], w=[("rep", c_)])
        wa_slots = [A.bf16(8192).rearrange("p (k c) -> p k c", k=16) for _ in range(2)]
        for cg in range(24):
            wi = cg % 2
            wa = wa_slots[wi]
            P.dma("pool", wa, w_ada[:, cg * 512:(cg + 1) * 512].rearrange("(kc p) c -> p kc c", p=128),
                  r=[], w=[("wa", wi)])
            P.dma("sp", brow[cg % 2], b_ada[cg * 512:(cg + 1) * 512].partition_broadcast(128),
                  r=[], w=[("brow", cg % 2)])
            for c_ in range(2):
                bi = (cg % 2) * 2 + c_
                for k_ in range(16):
                    P.op("pe", lambda e, bi=bi, c_=c_, k_=k_, wa=wa: e.matmul(
                        bank(bi), lhsT=rep[:, c_, k_, :], rhs=wa[:, k_, :], start=(k_ == 0), stop=(k_ == 15)),
                        r=[("rep", c_), ("wa", wi)], w=pk(bi))
                mr = mrow[bi]
                P.op("dve", lambda e, bi=bi, mr=mr, cg=cg: e.tensor_tensor(
                    out=mr, in0=bank(bi), in1=brow[cg % 2], op=ALU.add),
                    r=pk(bi) + [("brow", cg % 2)], w=[("mrow", bi)])
                P.dma("sp", modscr[c_:c_ + 1, cg * 512:(cg + 1) * 512], mr[0:1, :], r=[("mrow", bi)], w=[("modscr", c_, cg)])
        allmod = [("modscr", c_, cg) for c_ in range(2) for cg in range(24)]
        with nc.allow_non_contiguous_dma(reason="modulation vectors as columns"):
            for c_ in range(2):
                for vi, v_ in enumerate((0, 1, 3, 4)):
                    P.dma("sp", modc[:, c_, vi, :], modscr[c_, v_ * D:(v_ + 1) * D].rearrange("(k p) -> p k", p=128),
                          r=allmod, w=[("modc", c_, vi)], allow_slow_non_contiguous=True)
        for c_ in range(2):
            for ni, (vi, ncol, nk) in enumerate(((1, n1col, "n1col"), (3, n2col, "n2col"))):
                P.op("dve", lambda e, c_=c_, ni=ni, vi=vi: e.tensor_scalar(
                    out=acol[:, c_, ni, :], in0=modc[:, c_, vi, :], scalar1=1.0, scalar2=None, op0=ALU.add),
                    r=[("modc", c_, vi)], w=[("acol", c_, ni)])
                P.op("dve", lambda e, c_=c_, ni=ni, ncol=ncol: e.tensor_tensor(
                    out=acol[:, c_, ni, :], in0=acol[:, c_, ni, :], in1=ncol, op=ALU.mult),
                    r=[("acol", c_, ni), nk], w=[("acol", c_, ni)])
        P.barrier(full=True)
        A.release(m0)
        assert A.mark() == PERSIST_TOP
        issue_conv(N_CONV_FIRST)
        n_rest = len(conv_list)
        per_tile = (n_rest + max(nst, 1) - 1) // max(nst, 1)

        def norm_phase(get_src, ctx, ni, hbuf, hname, xn, ss, rt, rstd, junk):
            a_c = acol[:, ctx, ni, :]
            b_c = modc[:, ctx, 0 if ni == 0 else 2, :]
            srcs = {0: get_src(0), 1: get_src(1)}
            for tb in range(4):
                src, skeys = srcs[tb]
                P.op("act", lambda e, src=src, tb=tb: e.activation(
                    out=junk, in_=src, func=AF.Square, accum_out=ss[:, tb:tb + 1]), r=skeys, w=[("ss", tb)])
                P.op("act", lambda e, tb=tb: e.activation(
                    out=rt[:, tb:tb + 1], in_=ss[:, tb:tb + 1], func=AF.Sqrt, scale=1.0 / D, bias=epsc),
                    r=[("ss", tb), "epsc"], w=[("rt", tb)])
                P.op("dve", lambda e, tb=tb: e.reciprocal(out=rstd[:, tb:tb + 1], in_=rt[:, tb:tb + 1]),
                     r=[("rt", tb)], w=[("rstd", tb)])
                xb = xn[tb % 2]
                P.op("dve", lambda e, src=src, xb=xb, tb=tb: e.tensor_scalar(
                    out=xb, in0=src, scalar1=rstd[:, tb:tb + 1], scalar2=None, op0=ALU.mult),
                    r=skeys + [("rstd", tb)], w=[("xn", tb % 2)])
                if tb + 2 < 4:
                    srcs[tb + 2] = get_src(tb + 2)
                for g_ in range(2):
                    bi = (tb % 2) * 2 + g_
                    pb = bankb(bi).rearrange("p (k t) -> p k t", k=8)
                    for kk in range(8):
                        kc = g_ * 8 + kk
                        P.op("pe", lambda e, pb=pb, kk=kk, kc=kc, xb=xb: e.transpose(
                            out=pb[:, kk, :], in_=xb[:, kc * 128:(kc + 1) * 128], identity=ident),
                            r=[("xn", tb % 2), "ident"], w=pk(bi))
                    for kk in range(8):
                        kc = g_ * 8 + kk
                        if True:
                            P.op("dve", lambda e, pb=pb, kk=kk, kc=kc, tb=tb: e.tensor_scalar(
                                out=hbuf[:, kc, tb * 128:(tb + 1) * 128], in0=pb[:, kk, :],
                                scalar1=a_c[:, kc:kc + 1], scalar2=b_c[:, kc:kc + 1], op0=ALU.mult, op1=ALU.add),
                                r=pk(bi) + [("acol", ctx, ni), ("modc", ctx, 0 if ni == 0 else 2)],
                                w=[(hname, kc, tb)])
                        else:
                            P.op("act", lambda e, pb=pb, kk=kk, kc=kc, tb=tb: e.activation(
                                out=hbuf[:, kc, tb * 128:(tb + 1) * 128], in_=pb[:, kk, :], func=AF.Identity,
                                scale=a_c[:, kc:kc + 1], bias=b_c[:, kc:kc + 1]),
                                r=pk(bi) + [("acol", ctx, ni), ("modc", ctx, 0 if ni == 0 else 2)],
                                w=[(hname, kc, tb)])

        def hkeys(hname, kc=None):
            if kc is None:
                return [(hname, k, t) for k in range(16) for t in range(4)]
            return [(hname, kc, t) for t in range(4)]

        def proj_fm(bi, wv, wkey, hbuf, hname):
            for kc in range(16):
                P.op("pe", lambda e, kc=kc: e.matmul(bank(bi), lhsT=wv[:, kc, :], rhs=hbuf[:, kc, :],
                                                      start=(kc == 0), stop=(kc == 15)),
                     r=[wkey] + hkeys(hname, kc), w=pk(bi))

        def proj_tm(bset, wv, wkey, nk, kc0, act_ap, akeyfn, first, last):
            for kl in range(nk):
                kc = kc0 + kl
                for tb in range(4):
                    P.op("pe", lambda e, kl=kl, kc=kc, tb=tb: e.matmul(
                        bank(bset[tb]), lhsT=act_ap[:, kc, tb * 128:(tb + 1) * 128], rhs=wv[:, kl, :],
                        start=(first and kl == 0), stop=(last and kl == nk - 1)),
                        r=[wkey] + akeyfn(kc, tb), w=pk(bset[tb]))

        def mixer(kind, j, ctx, xsrc, prepass):
            P.barrier()
            mk = A.mark()
            assert mk == PERSIST_TOP
            xblk = [A.f32(D) for _ in range(2)]
            junk = A.bf16(D)
            xn = [A.bf16(D) for _ in range(2)]
            ss = A.f32(4); rt = A.f32(4); rstd = A.f32(4)
            invt = A.f32(512)
            assert A.mark() <= mk + 8192
            A.top = mk + 8192
            vtok = A.bf16(4 * 1024).rearrange("p (b c) -> p b c", b=4)
            h = A.bf16(16 * 512).rearrange("p (k t) -> p k t", k=16)
            assert A.mark() == mk + LOW_WORDS
            A.top = mk + LOW_WORDS
            omix = A.bf16(16 * 512).rearrange("p (k t) -> p k t", k=16)
            do_heads = True
            do_s3 = not (prepass and j == 0)
            if kind == "s" and do_s3:
                if j == nst - 1:
                    P.dma(sp_q, Sb, st[1].rearrange("h d v -> d h v"), r=[], w=["Sb"])
                elif not prepass:
                    P.dma(sp_q, Sb, sbscr[j], r=[("sbscr", j)], w=["Sb"])
                if (not prepass) and j == 0:
                    P.dma(sp_q, Sf, st[0].rearrange("h d v -> d h v"), r=[], w=["Sf"])

            X1KEYS = [("x1", tb_, cg_) for tb_ in range(4) for cg_ in range(4)]

            def get_src(tb):
                xb = xblk[tb % 2]
                P.dma(io_q, xb, xsrc[j * TT + tb * 128: j * TT + (tb + 1) * 128, :], r=[],
                      w=[("xblk", tb % 2)] + (X1KEYS if tb < 2 else []))
                return xb, [("xblk", tb % 2)]
            reuse = (kind == "s") and (not prepass)
            vkeys_all = [("vtok", tb, cg) for tb in range(4) for cg in range(2)]
            if reuse:
                P.dma(sp_q, h.rearrange("p k t -> p (k t)"), hscr[j], r=[("hscr", j)], w=hkeys("h") + hkeys("h2"))
                P.dma(io_q, vtok.rearrange("p b c -> p (b c)"), vscr[j], r=[("vscr", j)], w=vkeys_all)
            else:
                norm_phase(get_src, ctx, 0, h, "h", xn, ss, rt, rstd, junk)
                if prepass:
                    issue_conv(per_tile)
                    P.dma(io_q, hscr[j], h.rearrange("p k t -> p (k t)"), r=hkeys("h"), w=[("hscr", j)])

            def vproj():
                for cg in range(2):
                    wv, wkey = load_w(wq_v[cg].rearrange("p k c -> p (k c)"), 8192, ("wq_v", cg),
                                      "p (k c) -> p k c", k=16)
                    bset = [4 + t for t in range(4)] if prepass else [cg * 4 + t for t in range(4)]
                    proj_tm(bset, wv, wkey, 16, 0, h, lambda kc, tb: [("h", kc, tb)], True, True)
                    for tb in range(4):
                        P.op("act", lambda e, tb=tb, cg=cg, bset=bset: e.activation(
                            out=vtok[:, tb, cg * 512:(cg + 1) * 512], in_=bank(bset[tb]), func=AF.Copy),
                            r=pk(bset[tb]), w=[("vtok", tb, cg)])
                if prepass:
                    P.dma(io_q, vscr[j], vtok.rearrange("p b c -> p (b c)"), r=vkeys_all, w=[("vscr", j)])

            def pproj_prepass():
                pbuf = omix[:, 0:8, :]
                for s_ in range(2):
                    wv, wkey = load_w(wq_p[s_].rearrange("p t k c -> p (t k c)"), 8192, ("wq_p", s_),
                                      "p (t k c) -> p t k c", t=2, k=16)
                    for t_ in range(2):
                        for hh in range(2):
                            pc = s_ * 4 + t_ * 2 + hh
                            bi = 5 + pc % 3
                            proj_fm(bi, wv[:, t_, :, hh * 128:(hh + 1) * 128], wkey, h, "h")
                            P.op("act", lambda e, pc=pc, bi=bi: e.activation(out=pbuf[:, pc, :], in_=bank(bi), func=AF.Copy),
                                 r=pk(bi), w=[("pbuf", pc)])
                for pc in range(8):
                    P.dma(io_q, pscr[pc, :, j * TT:(j + 1) * TT], pbuf[:, pc, :], r=[("pbuf", pc)], w=[("pscr", pc, j)])
            if not reuse and not prepass:
                vproj()
            dm = None
            if not prepass:
                PA = Arena(arena_t[:], ARENA_WORDS)
                PA.top = mk
                alias = [("xblk", 0), ("xblk", 1), ("xn", 0), ("xn", 1)] + X1KEYS
                dm = PA.bf16(8 * 512).rearrange("p (k t) -> p k t", k=8)
                SH = [(1, 0), (1, 1), (2, 2), (4, 4)]

                def need_ranges(nlev, lo, hi):
                    rr = [None] * nlev
                    cur = (lo, hi)
                    for k_ in range(nlev - 1, -1, -1):
                        rr[k_] = cur
                        cur = (cur[0] - SH[k_][0], cur[1] + SH[k_][1])
                    return rr
                if kind == "p":
                    pbuf = PA.bf16(8 * 512).rearrange("p (k t) -> p k t", k=8)
                    CA = PA.f32(2 * 272).rearrange("p (r w) -> p r w", r=2)
                    CB = PA.f32(2 * 272).rearrange("p (r w) -> p r w", r=2)
                    nC, Rn = 256, 2
                else:
                    RA = PA.bf16(24 * 64).rearrange("p (r w) -> p r w", r=24)
                    RB = PA.f32(24 * 64).rearrange("p (r w) -> p r w", r=24)
                    RC = PA.f32(24 * 64).rearrange("p (r w) -> p r w", r=24)
                    CA = PA.f32(8 * 80).rearrange("p (r w) -> p r w", r=8)
                    CB = PA.f32(8 * 80).rearrange("p (r w) -> p r w", r=8)
                    nC, Rn = 64, 8
                assert PA.top <= mk + 7168
                P.op("pool", lambda e: e.memset(CA.rearrange("p r w -> p (r w)"), 0.0), w=["CA"] + alias)
                P.op("pool", lambda e: e.memset(CB.rearrange("p r w -> p (r w)"), 0.0), w=["CB"])
                if kind == "p":
                    for s_ in range(2):
                        wv, wkey = load_w(wq_p[s_].rearrange("p t k c -> p (t k c)"), 8192, ("wq_p", s_),
                                          "p (t k c) -> p t k c", t=2, k=16)
                        for t_ in range(2):
                            for hh in range(2):
                                pc = s_ * 4 + t_ * 2 + hh
                                bi = pc % 8
                                proj_fm(bi, wv[:, t_, :, hh * 128:(hh + 1) * 128], wkey, h, "h")
                                P.op("act", lambda e, pc=pc, bi=bi: e.activation(out=pbuf[:, pc, :], in_=bank(bi), func=AF.Copy),
                                     r=pk(bi), w=[("pbuf", pc)] + (alias if pc == 0 else []))
                else:
                    g_lo = max(0, 8 * j - 8)
                    g_hi = min(nst * 8, 8 * j + 16)
                    r_lo = g_lo - (8 * j - 8)
                    r_hi = r_lo + (g_hi - g_lo)
                    P.op("pool", lambda e: e.memset(RA.rearrange("p r w -> p (r w)"), 0.0), w=["RA"])
                for pc in range(8):
                    gi = pc // 2
                    nlev = gi + 1
                    if pc % 2 == 0:
                        src_iv = k_invp[gi] if kind == "p" else k_invs[gi, j * TT:(j + 1) * TT]
                        P.dma(io_q, invt, src_iv.partition_broadcast(128), r=[], w=["invt"])
                    if kind == "s":
                        P.dma(io_q, RA[:, r_lo:r_hi, :].rearrange("p r w -> p (r w)"), pscr[pc, :, g_lo * 64:g_hi * 64],
                              r=[], w=["RA"])
                        rr = need_ranges(nlev, 8, 16)
                        srcb, skey = RA, "RA"
                        for lv in range(nlev):
                            a_, b_ = SH[lv]
                            lo, hi = rr[lv]
                            dstb, dkey = (RB, "RB") if skey != "RB" else (RC, "RC")
                            P.op("pool", lambda e, srcb=srcb, dstb=dstb, a_=a_, b_=b_, lo=lo, hi=hi: e.tensor_tensor(
                                out=dstb[:, lo:hi, :], in0=srcb[:, lo - a_:hi - a_, :], in1=srcb[:, lo + b_:hi + b_, :],
                                op=ALU.add), r=[skey], w=[dkey])
                            srcb, skey = dstb, dkey
                        P.op("pool", lambda e, srcb=srcb: e.tensor_copy(out=CA[:, :, 8:72], in_=srcb[:, 8:16, :]),
                             r=[skey], w=["CA"])
                        rawp = RA[:, 8:16, :]
                        rawk = "RA"
                    else:
                        P.op("pool", lambda e, pc=pc: e.tensor_copy(
                            out=CA[:, :, 8:264], in_=pbuf[:, pc, :].rearrange("p (r t) -> p r t", r=2)),
                            r=[("pbuf", pc)], w=["CA"])
                        rawp = pbuf[:, pc, :].rearrange("p (r t) -> p r t", r=2)
                        rawk = ("pbuf", pc)
                    rr = need_ranges(nlev, 8, 8 + nC)
                    srcb, skey = CA, "CA"
                    for lv in range(nlev):
                        a_, b_ = SH[lv]
                        lo, hi = rr[lv]
                        dstb, dkey = (CB, "CB") if skey == "CA" else (CA, "CA")
                        P.op("pool", lambda e, srcb=srcb, dstb=dstb, a_=a_, b_=b_, lo=lo, hi=hi: e.tensor_tensor(
                            out=dstb[:, :, lo:hi], in0=srcb[:, :, lo - a_:hi - a_], in1=srcb[:, :, lo + b_:hi + b_],
                            op=ALU.add), r=[skey], w=[dkey])
                        srcb, skey = dstb, dkey
                    fin = srcb[:, :, 8:8 + nC]
                    iv = invt.rearrange("p (r t) -> p r t", r=Rn)
                    dmo = dm[:, pc, :].rearrange("p (r t) -> p r t", r=Rn)
                    P.op("pool", lambda e, fin=fin, iv=iv: e.tensor_tensor(out=fin, in0=fin, in1=iv, op=ALU.mult),
                         r=[skey, "invt"], w=[skey])
                    P.op("pool", lambda e, fin=fin, rawp=rawp, dmo=dmo: e.tensor_tensor(out=dmo, in0=fin, in1=rawp, op=ALU.subtract),
                         r=[skey, rawk], w=[("dm", pc)])
                    for bb, bk in ((CA, "CA"), (CB, "CB")):
                        P.op("pool", lambda e, bb=bb: e.memset(bb[:, :, 0:8], 0.0), r=[bk], w=[bk])
                        P.op("pool", lambda e, bb=bb: e.memset(bb[:, :, 8 + nC:16 + nC], 0.0), r=[bk], w=[bk])
            hm = A.mark()
            qs = [A.f32(512) for _ in range(2)]
            gs = [A.f32(512) for _ in range(2)]
            reuse = (kind == "s") and (not prepass)
            use_rec = (kind == "s")
            e1 = not reuse
            sg = [[A.f32(512), A.f32(512) if e1 else None] for _ in range(2)]
            fbuf = [A.f32(520), A.f32(520) if e1 else None]
            kbuf = [A.f32(512), A.f32(512) if e1 else None]
            Rb = [A.f32(520), A.f32(520) if (e1 and not prepass) else None]
            Pb = [[A.f32(520), None if use_rec else A.f32(520)] for _ in range(2)]
            qT = [[A.bf16(512) for _ in range(2)] for _ in range(2)]
            kT = [[A.bf16(512), None if use_rec else A.bf16(512)] for _ in range(2)]
            Rbb = [None, None]
            if use_rec:
                brec = [A.f32(1296) for _ in range(2)]
                for g_ in range(2):
                    kT[g_][1] = brec[g_][:, 0:256].bitcast(BF16)
                    Pb[g_][1] = brec[g_][:, 256:776]
                    Rbb[g_] = brec[g_][:, 776:1296]
            ktok = [A.bf16(2 * 4 * 128).rearrange("p (d b c) -> p d b c", d=2, b=4) for _ in range(2)]
            Sst = [A.bf16(8 * 128).rearrange("p (c v) -> p c v", c=8) for _ in range(2)]
            Amat = [A.bf16(4 * 128).rearrange("p (b t) -> p b t", b=4) for _ in range(2)]
            Ybuf = [[A.f32(128) for _ in range(2)] for _ in range(2)]
            Sout = [A.f32(128) for _ in range(4)]
            osq = A.bf16(512)
            ort = A.f32(512)
            onb = A.f32(512)
            for d_ in range(2):
                if fbuf[d_] is not None:
                    P.op("dve", lambda e, d_=d_: e.memset(fbuf[d_], 0.0), w=[("fbuf", d_)])
                P.op("dve", lambda e, d_=d_: e.memset(ktok[d_].rearrange("p d b c -> p (d b c)"), 0.0),
                     w=[("ktok", d_, 0), ("ktok", d_, 1)])
            nseg = 2 if kind == "p" else 1
            cps = 8 // nseg
            dirs = (1,) if prepass else (0, 1)
            edirs = (1,) if prepass else ((0,) if reuse else (0, 1))
            wkeep = {}

            def S1(hd):
                g = hd % 2
                hp, hh = hd // 2, hd % 2
                if prepass:
                    if hh == 0:
                        wkeep[1] = load_w(wq_head[hp, 1].rearrange("p t k c -> p (t k c)"), 8192,
                                          ("wq_head", hp, 1), "p (t k c) -> p t k c", t=2, k=16)
                    wv1, wkey1 = wkeep[1]
                    proj_fm(2, wv1[:, 0, :, hh * 128:(hh + 1) * 128], wkey1, h, "h")
                else:
                    if hh == 0:
                        wkeep[0] = load_w(wq_head[hp, 0].rearrange("p t k c -> p (t k c)"), 8192,
                                          ("wq_head", hp, 0), "p (t k c) -> p t k c", t=2, k=16)
                        wkeep[1] = load_w(wq_head[hp, 1].rearrange("p t k c -> p (t k c)"), 8192,
                                          ("wq_head", hp, 1), "p (t k c) -> p t k c", t=2, k=16)
                    wv0, wkey0 = wkeep[0]
                    wv1, wkey1 = wkeep[1]
                    proj_fm(0, wv0[:, 0, :, hh * 128:(hh + 1) * 128], wkey0, h, "h")
                    proj_fm(1, wv0[:, 1, :, hh * 128:(hh + 1) * 128], wkey0, h, "h")
                    if not reuse:
                        proj_fm(2, wv1[:, 0, :, hh * 128:(hh + 1) * 128], wkey1, h, "h")
                    else:
                        P.dma(sp_q, brec[g], bscr[j, hd], r=[("bscr", j, hd)], w=[("kT", g, 1), ("Pb", g, 1), ("Rbb", g)])
                    proj_fm(3, wv1[:, 1, :, hh * 128:(hh + 1) * 128], wkey1, h, "h")
                for d_ in edirs:
                    zb_ = 1 + d_
                    P.op("act", lambda e, d_=d_, zb_=zb_, g=g: e.activation(out=sg[g][d_], in_=bank(zb_), func=AF.Sigmoid),
                         r=pk(zb_), w=[("sg", g, d_)])
                if not prepass:
                    P.op("act", lambda e, g=g: e.activation(out=qs[g], in_=bank(0), func=AF.Silu), r=pk(0), w=[("qs", g)])
                    P.op("act", lambda e, g=g: e.activation(out=gs[g], in_=bank(3), func=AF.Silu), r=pk(3), w=[("gs", g)])

            def S2(hd):
                g = hd % 2
                if reuse:
                    r3b = Rbb[g].rearrange("p (c t) -> p c t", c=8)
                    q3b = qs[g].rearrange("p (c t) -> p c t", c=8)
                    qT3b = qT[g][1].rearrange("p (c t) -> p c t", c=8)
                    P.op("dve", lambda e, q3b=q3b, r3b=r3b, qT3b=qT3b: e.tensor_tensor(
                        out=qT3b, in0=q3b, in1=r3b[:, :, 0:64], op=ALU.mult), r=[("qs", g), ("Rbb", g)], w=[("qT", g, 1)])
                for d_ in edirs:
                    f3 = fbuf[d_].rearrange("p (c t) -> p c t", c=8)
                    s3 = sg[g][d_].rearrange("p (c t) -> p c t", c=8)
                    P.op("act", lambda e, d_=d_, f3=f3, s3=s3, hd=hd: e.activation(
                        out=f3[:, :, 1:65], in_=s3, func=AF.Identity, scale=omlc[:, d_, hd:hd + 1], bias=lbc[:, d_, hd:hd + 1]),
                        r=[("sg", g, d_), "omlc", "lbc"], w=[("fbuf", d_)])
                    P.op("act", lambda e, d_=d_, hd=hd, g=g: e.activation(
                        out=kbuf[d_], in_=sg[g][d_], func=AF.Identity, scale=nomlc[:, d_, hd:hd + 1], bias=omlc[:, d_, hd:hd + 1]),
                        r=[("sg", g, d_), "omlc", "nomlc"], w=[("kbuf", d_)])
                    P.op("dve", lambda e, d_=d_, g=g: e.tensor_tensor_scan(
                        out=Pb[g][d_], data0=fbuf[d_], data1=ind, initial=0.0, op0=ALU.mult, op1=ALU.max),
                        r=[("fbuf", d_), "ind"], w=[("Pb", g, d_)])
                    p3 = Pb[g][d_].rearrange("p (c t) -> p c t", c=8)
                    k3 = kbuf[d_].rearrange("p (c t) -> p c t", c=8)
                    kT3 = kT[g][d_].rearrange("p (c t) -> p c t", c=8)
                    if not prepass:
                        P.op("dve", lambda e, d_=d_, g=g: e.reciprocal(out=Rb[d_], in_=Pb[g][d_]), r=[("Pb", g, d_)], w=[("Rb", d_)])
                        r3 = Rb[d_].rearrange("p (c t) -> p c t", c=8)
                        q3 = qs[g].rearrange("p (c t) -> p c t", c=8)
                        qT3 = qT[g][d_].rearrange("p (c t) -> p c t", c=8)
                    if d_ == 0:
                        P.op("dve", lambda e, q3=q3, p3=p3, qT3=qT3: e.tensor_tensor(
                            out=qT3, in0=q3, in1=p3[:, :, 1:65], op=ALU.mult), r=[("qs", g), ("Pb", g, 0)], w=[("qT", g, 0)])
                        P.op("dve", lambda e, k3=k3, r3=r3, kT3=kT3: e.tensor_tensor(
                            out=kT3, in0=k3, in1=r3[:, :, 1:65], op=ALU.mult), r=[("kbuf", 0), ("Rb", 0)], w=[("kT", g, 0)])
                    else:
                        if not prepass:
                            P.op("dve", lambda e, q3=q3, r3=r3, qT3=qT3: e.tensor_tensor(
                                out=qT3, in0=q3, in1=r3[:, :, 0:64], op=ALU.mult), r=[("qs", g), ("Rb", 1)], w=[("qT", g, 1)])
                        P.op("dve", lambda e, k3=k3, p3=p3, kT3=kT3: e.tensor_tensor(
                            out=kT3, in0=k3, in1=p3[:, :, 0:64], op=ALU.mult), r=[("kbuf", 1), ("Pb", g, 1)], w=[("kT", g, 1)])
                        if prepass:
                            P.op("dve", lambda e, g=g: e.reciprocal(out=Rbb[g], in_=Pb[g][1]), r=[("Pb", g, 1)], w=[("Rbb", g)])
                            P.dma(io_q, bscr[j, hd], brec[g], r=[("kT", g, 1), ("Pb", g, 1), ("Rbb", g)], w=[("bscr", j, hd)])

            def S3(hd):
                g = hd % 2
                pkb = bankb(4).rearrange("p (d b c) -> p d b c", d=2, b=4)
                for d_ in dirs:
                    for tb in range(4):
                        P.op("pe", lambda e, d_=d_, tb=tb, g=g: e.transpose(
                            out=pkb[:, d_, tb, :], in_=kT[g][d_][:, tb * 128:(tb + 1) * 128], identity=ident),
                            r=[("kT", g, d_), "ident"], w=pk(4))
                for d_ in dirs:
                    for half in range(2):
                        hs_ = slice(half * 64, (half + 1) * 64)
                        P.op("act", lambda e, d_=d_, half=half, hs_=hs_: e.activation(
                            out=ktok[half][hs_, d_], in_=pkb[hs_, d_], func=AF.Copy),
                            r=pk(4), w=[("ktok", half, d_)])
                for d_ in dirs:
                    for c in range(8):
                        tb, half = c // 2, c % 2
                        bi = d_ * 2 + c // 4
                        ub = bank(bi).rearrange("p (c v) -> p c v", c=4)
                        P.op("pe", lambda e, d_=d_, tb=tb, half=half, ub=ub, c=c, hd=hd: e.matmul(
                            ub[:, c % 4, :], lhsT=ktok[half][:, d_, tb, :],
                            rhs=vtok[:, tb, hd * 128:(hd + 1) * 128], start=True, stop=True),
                            r=[("ktok", half, d_), ("vtok", tb, hd // 4)], w=pk(bi))
                if not prepass:
                    for d_ in range(2):
                        sb_ = bank(5 + d_).rearrange("p (b t) -> p b t", b=4)
                        for tb in range(4):
                            P.op("pe", lambda e, d_=d_, tb=tb, sb_=sb_, g=g: e.matmul(
                                sb_[:, tb, :], lhsT=kT[g][d_][:, tb * 128:(tb + 1) * 128],
                                rhs=qT[g][d_][:, tb * 128:(tb + 1) * 128],
                                start=True, stop=True), r=[("kT", g, d_), ("qT", g, d_)], w=pk(5 + d_))
                chains = {0: [], 1: []}
                for d_ in dirs:
                    def Q(eng, fn, r=(), w=(), d_=d_):
                        chains[d_].append(lambda: P.op(eng, fn, r=r, w=w))

                    def Qd(q, out, in_, r=(), w=(), d_=d_):
                        chains[d_].append(lambda: P.dma(q, out, in_, r=r, w=w))
                    p3 = Pb[g][d_].rearrange("p (c t) -> p c t", c=8)
                    pbk = ("Pb", g, d_)

                    def U(c, d_=d_):
                        bi = d_ * 2 + c // 4
                        return bank(bi).rearrange("p (c v) -> p c v", c=4)[:, c % 4, :], ("ps", bi)

                    def Dc(c, p3=p3):
                        return p3[:, c, 64:65]
                    for sgm in range(nseg):
                        c0 = sgm * cps
                        if d_ == 0:
                            if kind == "p":
                                Q("act", lambda e, c0=c0: e.activation(out=Sst[0][:, c0, :], in_=zerob, func=AF.Copy),
                                     r=["zerob"], w=[("Sst", 0, c0)])
                                u, uk = U(c0)
                                Q("dve", lambda e, u=u: e.tensor_copy(out=Ybuf[0][0], in_=u), r=[uk], w=[("Y", 0, 0)])
                            else:
                                Q("act", lambda e, c0=c0, hd=hd: e.activation(out=Sst[0][:, c0, :], in_=Sf[:, hd, :], func=AF.Copy),
                                     r=["Sf"], w=[("Sst", 0, c0)])
                                u, uk = U(c0)
                                Q("dve", lambda e, u=u, hd=hd: e.tensor_tensor(out=Ybuf[0][0], in0=u, in1=Sf[:, hd, :], op=ALU.add),
                                     r=[uk, "Sf"], w=[("Y", 0, 0)])
                            cur = 0
                            for cc in range(cps):
                                c = c0 + cc
                                if cc < cps - 1:
                                    Q("act", lambda e, c=c, cur=cur, dc=Dc(c): e.activation(
                                        out=Sst[0][:, c + 1, :], in_=Ybuf[0][cur], func=AF.Identity, scale=dc),
                                        r=[("Y", 0, cur), pbk], w=[("Sst", 0, c + 1)])
                                    u, uk = U(c + 1)
                                    Q("dve", lambda e, c=c, cur=cur, u=u, dc=Dc(c): e.scalar_tensor_tensor(
                                        out=Ybuf[0][1 - cur], in0=Ybuf[0][cur], scalar=dc, in1=u, op0=ALU.mult, op1=ALU.add),
                                        r=[("Y", 0, cur), pbk, uk], w=[("Y", 0, 1 - cur)])
                                    cur = 1 - cur
                                else:
                                    if kind == "p":
                                        so = Sout[sgm * 2]
                                        Q("dve", lambda e, c=c, cur=cur, so=so, dc=Dc(c): e.tensor_scalar(
                                            out=so, in0=Ybuf[0][cur], scalar1=dc, scalar2=None, op0=ALU.mult),
                                            r=[("Y", 0, cur), pbk], w=[("Sout", sgm * 2)])
                                        Qd(sp_q, ns[j * 2 + sgm, 0, hd], so, r=[("Sout", sgm * 2)], w=[("ns", j, sgm, 0, hd)])
                                    else:
                                        Q("dve", lambda e, c=c, cur=cur, hd=hd, dc=Dc(c): e.tensor_scalar(
                                            out=Sf[:, hd, :], in0=Ybuf[0][cur], scalar1=dc, scalar2=None, op0=ALU.mult),
                                            r=[("Y", 0, cur), pbk, "Sf"], w=["Sf"])
                        else:
                            cur = 0
                            first = True
                            for cc in range(cps - 1, -1, -1):
                                c = c0 + cc
                                u, uk = U(c)
                                if first and kind == "p":
                                    if not prepass:
                                        Q("act", lambda e, c=c: e.activation(out=Sst[1][:, c, :], in_=zerob, func=AF.Copy),
                                             r=["zerob"], w=[("Sst", 1, c)])
                                    Q("dve", lambda e, u=u: e.tensor_copy(out=Ybuf[1][0], in_=u), r=[uk], w=[("Y", 1, 0)])
                                    cur = 0
                                elif first:
                                    if not prepass:
                                        Q("act", lambda e, c=c, hd=hd, dc=Dc(c): e.activation(
                                            out=Sst[1][:, c, :], in_=Sb[:, hd, :], func=AF.Identity, scale=dc),
                                            r=["Sb", pbk], w=[("Sst", 1, c)])
                                    Q("dve", lambda e, c=c, u=u, hd=hd, dc=Dc(c): e.scalar_tensor_tensor(
                                        out=Ybuf[1][0], in0=Sb[:, hd, :], scalar=dc, in1=u, op0=ALU.mult, op1=ALU.add),
                                        r=["Sb", pbk, uk], w=[("Y", 1, 0)])
                                    cur = 0
                                else:
                                    if not prepass:
                                        Q("act", lambda e, c=c, cur=cur, dc=Dc(c): e.activation(
                                            out=Sst[1][:, c, :], in_=Ybuf[1][cur], func=AF.Identity, scale=dc),
                                            r=[("Y", 1, cur), pbk], w=[("Sst", 1, c)])
                                    Q("dve", lambda e, c=c, cur=cur, u=u, dc=Dc(c): e.scalar_tensor_tensor(
                                        out=Ybuf[1][1 - cur], in0=Ybuf[1][cur], scalar=dc, in1=u, op0=ALU.mult, op1=ALU.add),
                                        r=[("Y", 1, cur), pbk, uk], w=[("Y", 1, 1 - cur)])
                                    cur = 1 - cur
                                first = False
                            if kind == "p":
                                so = Sout[sgm * 2 + 1]
                                Q("dve", lambda e, cur=cur, so=so: e.tensor_copy(out=so, in_=Ybuf[1][cur]),
                                     r=[("Y", 1, cur)], w=[("Sout", sgm * 2 + 1)])
                                Qd(sp_q, ns[j * 2 + sgm, 1, hd], so, r=[("Sout", sgm * 2 + 1)], w=[("ns", j, sgm, 1, hd)])
                            elif prepass:
                                Q("dve", lambda e, cur=cur, hd=hd: e.tensor_copy(out=Sb[:, hd, :], in_=Ybuf[1][cur]),
                                     r=[("Y", 1, cur), "Sb"], w=["Sb"])
                for i_ in range(max(len(chains[0]), len(chains[1]))):
                    for d_ in dirs:
                        if i_ < len(chains[d_]):
                            chains[d_][i_]()
                if prepass:
                    return
                for d_ in range(2):
                    sb_ = bank(5 + d_).rearrange("p (b t) -> p b t", b=4)
                    msk = maskf if d_ == 0 else maskb
                    P.op("dve", lambda e, d_=d_, sb_=sb_, msk=msk: e.tensor_tensor(
                        out=Amat[d_], in0=sb_, in1=msk.unsqueeze(1).to_broadcast([128, 4, 128]), op=ALU.mult),
                        r=pk(5 + d_) + ["maskf", "maskb"], w=[("Amat", d_, tb) for tb in range(4)])
                for c in range(8):
                    tb, half = c // 2, c % 2
                    ob = bank(7)[:, c * 64:(c + 1) * 64]
                    hs = slice(half * 64, (half + 1) * 64)
                    for d_ in range(2):
                        P.op("pe", lambda e, d_=d_, tb=tb, hs=hs, ob=ob, hd=hd: e.matmul(
                            ob, lhsT=vtok[:, tb, hd * 128:(hd + 1) * 128], rhs=Amat[d_][:, tb, hs],
                            start=(d_ == 0), stop=False),
                            r=[("vtok", tb, hd // 4), ("Amat", d_, tb)], w=pk(7))
                    for d_ in range(2):
                        P.op("pe", lambda e, d_=d_, c=c, ob=ob, g=g: e.matmul(
                            ob, lhsT=Sst[d_][:, c, :], rhs=qT[g][d_][:, c * 64:(c + 1) * 64], start=False, stop=(d_ == 1)),
                            r=[("Sst", d_, c), ("qT", g, d_)], w=pk(7))
                P.op("act", lambda e: e.activation(out=osq, in_=bank(7), func=AF.Square), r=pk(7), w=["osq"])
                P.op("pe", lambda e: e.matmul(bank(4), lhsT=ones, rhs=osq, start=True, stop=True),
                     r=["ones", "osq"], w=pk(4))
                P.op("act", lambda e: e.activation(out=ort, in_=bank(4), func=AF.Ln, scale=1.0 / 128, bias=epsc),
                     r=pk(4) + ["epsc"], w=["ort"])
                P.op("act", lambda e: e.activation(out=ort, in_=ort, func=AF.Exp, scale=-0.5), r=["ort"], w=["ort"])
                P.op("dve", lambda e: e.tensor_tensor(out=onb, in0=bank(7), in1=ort, op=ALU.mult),
                     r=pk(7) + ["ort"], w=["onb"])
                P.op("dve", lambda e, hd=hd, g=g: e.scalar_tensor_tensor(
                    out=omix[:, hd, :], in0=onb, scalar=hgw, in1=gs[g], op0=ALU.mult, op1=ALU.mult),
                    r=["onb", "hgw", ("gs", g)], w=[("omix", hd)])

            if do_heads:
                S1(0)
                S2(0)
                if prepass:
                    vproj()
                for hd in range(HEADS):
                    if hd + 1 < HEADS:
                        S1(hd + 1)
                    if prepass and hd == 0:
                        pproj_prepass()
                    if do_s3:
                        S3(hd)
                    if hd + 1 < HEADS:
                        S2(hd + 1)
            if prepass:
                if j >= 1:
                    P.dma(io_q, sbscr[j - 1], Sb, r=["Sb"], w=[("sbscr", j - 1)])
            if prepass:
                P.barrier()
            A.release(hm)
            if prepass:
                P.barrier()
                A.release(mk)
                return None
            wpl, wplkey = load_w(wq_pool.rearrange("p g k c -> p (g k c)"), 2048, "wq_pool_all",
                                 "p (g k c) -> p g k c", g=4, k=2)
            for gi in range(4):
                for half in range(2):
                    oc = gi * 2 + half
                    bi = oc % 4
                    for kc in range(2):
                        P.op("pe", lambda e, gi=gi, half=half, kc=kc, bi=bi: e.matmul(
                            bank(bi), lhsT=wpl[:, gi, kc, half * 128:(half + 1) * 128], rhs=dm[:, gi * 2 + kc, :],
                            start=(kc == 0), stop=(kc == 1)), r=[wplkey, ("dm", gi * 2 + kc)], w=pk(bi))
                    P.op("act", lambda e, oc=oc, bi=bi: e.activation(
                        out=omix[:, 8 + oc, :], in_=bank(bi), func=AF.Identity, scale=pscol[:, oc:oc + 1]),
                        r=pk(bi) + ["pscol"], w=[("omix", 8 + oc)])
            return mk, omix

        def tile_main(kind, j, ctx, xsrc, ydst, last_tile):
            res = mixer(kind, j, ctx, xsrc, False)
            mk, omix = res
            P.barrier()
            A.release(mk)
            x1 = A.f32(4 * D).rearrange("p (b c) -> p b c", b=4)
            gbc = A.f32(D)
            h2 = A.bf16(16 * 512).rearrange("p (k t) -> p k t", k=16)
            assert A.mark() == mk + LOW_WORDS
            A.top = mk + LOW_WORDS + 4096
            tmp = [A.f32(512) for _ in range(2)]
            junk = A.bf16(D)
            xn = [A.bf16(D) for _ in range(2)]
            ss = A.f32(4); rt = A.f32(4); rstd = A.f32(4)
            for tb in range(4):
                P.dma(io_q, x1[:, tb, :], xsrc[j * TT + tb * 128: j * TT + (tb + 1) * 128, :], r=[],
                      w=[("x1", tb, cg) for cg in range(4)])
            P.dma(io_q, gbc, modscr[ctx, 2 * D:3 * D].partition_broadcast(128), r=[], w=["gbc"])

            def resid(bset, cg, gv, gkey, tmpl):
                for tb in range(4):
                    t_ = tmpl[tb % 2]
                    P.op("dve", lambda e, tb=tb, t_=t_, cg=cg, bset=bset: e.tensor_tensor(
                        out=t_, in0=bank(bset[tb]), in1=gv[:, cg * 512:(cg + 1) * 512], op=ALU.mult),
                        r=pk(bset[tb]) + [gkey], w=[("tmp", tb % 2)])
                    P.op("dve", lambda e, tb=tb, t_=t_, cg=cg: e.tensor_tensor(
                        out=x1[:, tb, cg * 512:(cg + 1) * 512], in0=x1[:, tb, cg * 512:(cg + 1) * 512], in1=t_, op=ALU.add),
                        r=[("tmp", tb % 2), ("x1", tb, cg)], w=[("x1", tb, cg)])

            for cg in range(4):
                wv, wkey = load_w(wq_out[cg].rearrange("p k c -> p (k c)"), 8192, ("wq_out", cg), "p (k c) -> p k c", k=16)
                bset = [(cg % 2) * 4 + t for t in range(4)]
                proj_tm(bset, wv, wkey, 16, 0, omix, lambda kc, tb: [("omix", kc)], True, True)
                resid(bset, cg, gbc, "gbc", tmp)

            def get_src(tb):
                return x1[:, tb, :], [("x1", tb, cg) for cg in range(4)]
            norm_phase(get_src, ctx, 1, h2, "h2", xn, ss, rt, rstd, junk)
            if kind == "s" and not last_tile:
                P.dma(io_q, x1def[j:j + 1, :], x1[127:128, 3, :], r=[("x1", 3, cg) for cg in range(4)], w=[("x1def", j)])
            b2_alias = [("omix", kc_) for kc_ in range(16)] + [("tmp", 0), ("tmp", 1), ("xn", 0), ("xn", 1)]
            A.top = mk + LOW_WORDS
            act = A.bf16(NFC * 512).rearrange("p (k t) -> p k t", k=NFC)
            acc = [[A.f32(512) for _ in range(2)] for _ in range(2)]
            sa = [A.f32(512) for _ in range(2)]
            tmp2 = [A.f32(512) for _ in range(2)]
            fbc = A.f32(D)
            junk2 = A.bf16(D)
            ss2 = A.f32(4); rt2 = A.f32(4); rs2 = A.f32(4)
            nseq = 2 if kind == "p" else 1
            L_ = 512 // nseq
            for i in range(NFC):
                s_, hh = i // 2, i % 2
                if hh == 0:
                    wv, wkey = load_w(wq_up[s_].rearrange("p t k c -> p (t k c)"), 8192, ("wq_up_all", s_),
                                      "p (t k c) -> p t k c", t=2, k=16)
                    wkeep = (wv, wkey)
                wv, wkey = wkeep
                pair = (i % 4) * 2
                for ab in range(2):
                    bi = pair + ab
                    fc = i + ab * NFC
                    proj_fm(bi, wv[:, ab, :, hh * 128:(hh + 1) * 128], wkey, h2, "h2")
                    ac = acc[i % 2][ab]
                    akey = ("acc", i % 2, ab)
                    P.op("act", lambda e, bi=bi, ac=ac, fc=fc: e.activation(
                        out=ac, in_=bank(bi), func=AF.Identity, scale=cwcol[:, 1, fc:fc + 1], bias=cbcol[:, fc:fc + 1]),
                        r=pk(bi) + [("cwcol", 1), "cbcol"], w=[akey])
                    a3 = ac.rearrange("p (s t) -> p s t", s=nseq)
                    b3 = bank(bi).rearrange("p (s t) -> p s t", s=nseq)
                    P.op("dve", lambda e, a3=a3, b3=b3, fc=fc: e.scalar_tensor_tensor(
                        out=a3[:, :, 1:L_], in0=b3[:, :, 0:L_ - 1], scalar=cwcol[:, 0, fc:fc + 1], in1=a3[:, :, 1:L_],
                        op0=ALU.mult, op1=ALU.add), r=pk(bi) + [akey, ("cwcol", 0)], w=[akey])
                    P.op("dve", lambda e, a3=a3, b3=b3, fc=fc: e.scalar_tensor_tensor(
                        out=a3[:, :, 0:L_ - 1], in0=b3[:, :, 1:L_], scalar=cwcol[:, 2, fc:fc + 1], in1=a3[:, :, 0:L_ - 1],
                        op0=ALU.mult, op1=ALU.add), r=pk(bi) + [akey, ("cwcol", 2)], w=[akey])
                    if kind == "s":
                        if j >= 1:
                            P.op("dve", lambda e, ac=ac, fc=fc: e.scalar_tensor_tensor(
                                out=ac[:, 0:1], in0=udef[:, fc, j - 1, 1:2], scalar=cwcol[:, 0, fc:fc + 1], in1=ac[:, 0:1],
                                op0=ALU.mult, op1=ALU.add), r=[akey, ("udef", fc, j - 1, 1), ("cwcol", 0)], w=[akey])
                            P.op("act", lambda e, bi=bi, fc=fc: e.activation(
                                out=udef[:, fc, j - 1, 2:3], in_=bank(bi)[:, 0:1], func=AF.Copy),
                                r=pk(bi, 0, 1), w=[("udef", fc, j - 1, 2)])
                        if not last_tile:
                            P.op("act", lambda e, bi=bi, fc=fc: e.activation(
                                out=udef[:, fc, j, 0:2], in_=bank(bi)[:, 510:512], func=AF.Copy),
                                r=pk(bi, 3, 4), w=[("udef", fc, j, 1)])
                P.op("act", lambda e, i=i: e.activation(out=sa[i % 2], in_=acc[i % 2][0], func=AF.Silu),
                     r=[("acc", i % 2, 0)], w=[("sa", i % 2)])
                P.op("dve", lambda e, i=i: e.tensor_tensor(out=act[:, i, :], in0=sa[i % 2], in1=acc[i % 2][1], op=ALU.mult),
                     r=[("sa", i % 2), ("acc", i % 2, 1)], w=[("act", i)] + (b2_alias if i == 0 else []))
            P.dma(io_q, gbc, modscr[ctx, 5 * D:6 * D].partition_broadcast(128), r=["gbc"], w=["gbc"])
            P.dma(io_q, fbc, final_norm_w.partition_broadcast(128), r=[], w=["fbc"])
            for cg in range(4):
                bset = [(cg % 2) * 4 + t for t in range(4)]
                for part in range(4):
                    wv, wkey = load_w(wq_down[cg, part].rearrange("p k c -> p (k c)"), 11 * 512, ("wq_down", cg, part),
                                      "p (k c) -> p k c", k=11)
                    proj_tm(bset, wv, wkey, 11, part * 11, act, lambda kc, tb: [("act", kc)], part == 0, part == 3)
                resid(bset, cg, gbc, "gbc", tmp2)
            for tb in range(4):
                xk = [("x1", tb, cg) for cg in range(4)]
                P.op("act", lambda e, tb=tb: e.activation(out=junk2, in_=x1[:, tb, :], func=AF.Square,
                                                          accum_out=ss2[:, tb:tb + 1]), r=xk, w=[("ss2", tb)])
                P.op("act", lambda e, tb=tb: e.activation(out=rt2[:, tb:tb + 1], in_=ss2[:, tb:tb + 1], func=AF.Sqrt,
                                                          scale=1.0 / D, bias=epsc), r=[("ss2", tb), "epsc"], w=[("rt2", tb)])
                P.op("dve", lambda e, tb=tb: e.reciprocal(out=rs2[:, tb:tb + 1], in_=rt2[:, tb:tb + 1]),
                     r=[("rt2", tb)], w=[("rs2", tb)])
                P.op("dve", lambda e, tb=tb: e.scalar_tensor_tensor(
                    out=x1[:, tb, :], in0=x1[:, tb, :], scalar=rs2[:, tb:tb + 1], in1=fbc, op0=ALU.mult, op1=ALU.mult),
                    r=xk + [("rs2", tb), "fbc"], w=xk)
                nrow = 128
                if kind == "s" and not last_tile and tb == 3:
                    nrow = 127
                P.dma(io_q, ydst[j * TT + tb * 128: j * TT + tb * 128 + nrow, :], x1[0:nrow, tb, :], r=xk, w=[("y", kind, j, tb)],
                      bg=True)
            P.barrier()
            A.release(mk)

        def deferred():
            nb = nst - 1
            if nb <= 0:
                return
            P.barrier(full=True)
            mk = A.mark()
            actd = A.bf16(NFC * 8).rearrange("p (k t) -> p k t", k=NFC)
            cv = A.f32(88 * 8).rearrange("p (c j) -> p c j", c=88)
            t1 = A.f32(88 * 8).rearrange("p (c j) -> p c j", c=88)
            sad = A.f32(NFC * 8).rearrange("p (c j) -> p c j", c=NFC)
            xd = A.f32(D)
            g2 = A.f32(D)
            fb = A.f32(D)
            junk = A.bf16(D)
            tmpd = A.f32(512)
            ssd = A.f32(1); rtd = A.f32(1); rsd = A.f32(1)
            ctx = 1
            alld = []
            for jj in range(nb):
                P.op("dve", lambda e, jj=jj: e.tensor_tensor(out=cv[:, :, jj], in0=udef[:, :, jj, 0], in1=cwcol[:, 0, :], op=ALU.mult),
                     r=alld + [("cwcol", 0)], w=[("cv", jj)])
                for e_ in (1, 2):
                    P.op("dve", lambda e, jj=jj, e_=e_: e.tensor_tensor(out=t1[:, :, jj], in0=udef[:, :, jj, e_], in1=cwcol[:, e_, :], op=ALU.mult),
                         r=alld + [("cwcol", e_)], w=[("t1", jj)])
                    P.op("dve", lambda e, jj=jj: e.tensor_tensor(out=cv[:, :, jj], in0=cv[:, :, jj], in1=t1[:, :, jj], op=ALU.add),
                         r=[("cv", jj), ("t1", jj)], w=[("cv", jj)])
                P.op("dve", lambda e, jj=jj: e.tensor_tensor(out=cv[:, :, jj], in0=cv[:, :, jj], in1=cbcol, op=ALU.add),
                     r=[("cv", jj), "cbcol"], w=[("cv", jj)])
                P.op("act", lambda e, jj=jj: e.activation(out=sad[:, :, jj], in_=cv[:, 0:NFC, jj], func=AF.Silu),
                     r=[("cv", jj)], w=[("sad", jj)])
                P.op("dve", lambda e, jj=jj: e.tensor_tensor(out=actd[:, :, jj], in0=sad[:, :, jj], in1=cv[:, NFC:88, jj], op=ALU.mult),
                     r=[("sad", jj), ("cv", jj)], w=["actd"])
            P.dma(io_q, xd[0:nb, :], x1def[0:nb, :], r=[("x1def", jj) for jj in range(nb)], w=["xd"])
            P.dma(io_q, g2, modscr[ctx, 5 * D:6 * D].partition_broadcast(128), r=[], w=["g2d"])
            P.dma(io_q, fb, final_norm_w.partition_broadcast(128), r=[], w=["fbd"])
            for cg in range(4):
                bi = cg
                for part in range(4):
                    wv, wkey = load_w(wq_down[cg, part].rearrange("p k c -> p (k c)"), 11 * 512, ("wq_down", cg, part),
                                      "p (k c) -> p k c", k=11)
                    for kl in range(11):
                        kc = part * 11 + kl
                        P.op("pe", lambda e, kl=kl, kc=kc, wv=wv, bi=bi, part=part: e.matmul(
                            bank(bi)[0:nb, :], lhsT=actd[:, kc, 0:nb], rhs=wv[:, kl, :],
                            start=(part == 0 and kl == 0), stop=(part == 3 and kl == 10)),
                            r=[wkey, "actd"], w=pk(bi))
                P.op("dve", lambda e, bi=bi, cg=cg: e.tensor_tensor(
                    out=tmpd[0:nb, :], in0=bank(bi)[0:nb, :], in1=g2[0:nb, cg * 512:(cg + 1) * 512], op=ALU.mult),
                    r=pk(bi) + ["g2d"], w=["tmpd"])
                P.op("dve", lambda e, cg=cg: e.tensor_tensor(
                    out=xd[0:nb, cg * 512:(cg + 1) * 512], in0=xd[0:nb, cg * 512:(cg + 1) * 512], in1=tmpd[0:nb, :], op=ALU.add),
                    r=["tmpd", "xd"], w=["xd"])
            P.op("act", lambda e: e.activation(out=junk[0:nb, :], in_=xd[0:nb, :], func=AF.Square, accum_out=ssd[0:nb, :]),
                 r=["xd"], w=["ssd"])
            P.op("act", lambda e: e.activation(out=rtd[0:nb, :], in_=ssd[0:nb, :], func=AF.Sqrt, scale=1.0 / D, bias=epsc[0:nb, :]),
                 r=["ssd", "epsc"], w=["rtd"])
            P.op("dve", lambda e: e.reciprocal(out=rsd[0:nb, :], in_=rtd[0:nb, :]), r=["rtd"], w=["rsd"])
            P.op("dve", lambda e: e.scalar_tensor_tensor(
                out=xd[0:nb, :], in0=xd[0:nb, :], scalar=rsd[0:nb, :], in1=fb[0:nb, :], op0=ALU.mult, op1=ALU.mult),
                r=["xd", "rsd", "fbd"], w=["xd"])
            for jj in range(nb):
                P.dma(io_q, ys[jj * TT + 511: jj * TT + 512, :], xd[jj:jj + 1, :], r=["xd"], w=[("ydef", jj)])
            P.barrier()
            A.release(mk)

        if stages >= 1:
            for j in range(nst - 1, -1, -1):
                mixer("s", j, 1, xs, True)
        issue_conv(len(conv_list))
        if stages >= 2:
            for j in range(npt):
                tile_main("p", j, 0, xp, yp, True)
        if stages >= 3:
            for j in range(nst):
                tile_main("s", j, 1, xs, ys, j == nst - 1)
            deferred()
        counts = P.emit(csem, dsem)
    return nc, counts


_CACHE = {}


def _get_program(npt, nst):
    key = (npt, nst)
    if key not in _CACHE:
        _CACHE[key] = build_program(npt, nst)
    return _CACHE[key]


def make_in_map(core, npt, nst, x_prompt, x_sample, state_hgrn, c, c_ctx, w_ada, b_ada, norm1_w, w_in,
                lb_param, hg_norm_w, w_pool, pool_scale, w_out, norm2_w, w_up, conv_w, conv_b, w_down,
                final_norm_w, consts):
    f = lambda a: np.ascontiguousarray(np.asarray(a, dtype=np.float32))
    nseq = npt * 2
    m = {}
    if npt > 0:
        m["xp"] = f(x_prompt[core * nseq:(core + 1) * nseq]).reshape(nseq * 256, D)
    else:
        m["xp"] = np.zeros((TT, D), np.float32)
    m["xs"] = f(x_sample[core]).reshape(-1, D)
    m["st"] = f(state_hgrn[core, 0])
    m["cvec"] = np.ascontiguousarray(np.stack([np.asarray(c_ctx, np.float32), np.asarray(c[core], np.float32)]))
    m["w_ada"] = f(w_ada[0]); m["b_ada"] = f(b_ada[0]); m["norm1_w"] = f(norm1_w[0]); m["w_in"] = f(w_in[0])
    m["lb_param"] = f(lb_param); m["hg_norm_w"] = f(hg_norm_w[0]); m["w_pool"] = f(w_pool[0])
    m["pool_scale"] = f(pool_scale[0]); m["w_out"] = f(w_out[0]); m["norm2_w"] = f(norm2_w[0])
    m["w_up"] = f(w_up[0]); m["conv_w"] = f(conv_w[0]); m["conv_b"] = f(conv_b[0]); m["w_down"] = f(w_down[0])
    m["final_norm_w"] = f(final_norm_w)
    m.update(consts)
    return m


def kernel(x_prompt, x_sample, state_hgrn, c, c_ctx, w_ada, b_ada, norm1_w, w_in, lb_param, hg_norm_w,
           w_pool, pool_scale, w_out, norm2_w, w_up, conv_w, conv_b, w_down, final_norm_w):
    ncores = 8
    x_prompt = np.asarray(x_prompt); x_sample = np.asarray(x_sample)
    B, S, _ = x_prompt.shape
    DB, DS, _ = x_sample.shape
    assert DB == ncores and B % ncores == 0 and S == 256
    npt = (B // ncores) // 2
    nst = DS // TT
    nc, _ = _get_program(npt, nst)
    consts = host_constants(nst)
    in_maps = [make_in_map(k, npt, nst, x_prompt, x_sample, np.asarray(state_hgrn), np.asarray(c), np.asarray(c_ctx),
                           np.asarray(w_ada), np.asarray(b_ada), np.asarray(norm1_w), np.asarray(w_in),
                           np.asarray(lb_param), np.asarray(hg_norm_w), np.asarray(w_pool), np.asarray(pool_scale),
                           np.asarray(w_out), np.asarray(norm2_w), np.asarray(w_up), np.asarray(conv_w),
                           np.asarray(conv_b), np.asarray(w_down), np.asarray(final_norm_w), consts)
               for k in range(ncores)]
    res = run_bass_kernel_spmd(nc, in_maps, core_ids=list(range(ncores)))
    outs = res.results
    nseq = npt * 2
    y_prompt = np.concatenate([np.asarray(o["yp"], np.float32).reshape(nseq, 256, D) for o in outs], axis=0)
    y_sample = np.stack([np.asarray(o["ys"], np.float32).reshape(DS, D) for o in outs], axis=0)
    new_state = np.concatenate([np.asarray(o["ns"], np.float32).reshape(nseq, 1, 2, HEADS, 128, 128) for o in outs], axis=0)
    return (y_prompt, y_sample, new_state)
```
